# Optimizing a Trainium2 kernel written in Bass

```python
import math
import jax, jax.numpy as jnp
from jax import lax
import numpy as np

D_MODEL = 1024
BATCH = 8
SEQ = 4096
DEPTH = 4

GROUP_WIDTH = D_MODEL // 4
N_GROUPS = 5
MIX_WIDTH = N_GROUPS * GROUP_WIDTH
POOL_WINDOWS = (2, 4, 8, 16)
POOL_CH = GROUP_WIDTH // len(POOL_WINDOWS)
CONV_WIDTH = 31
SHORT_CONV_WIDTH = 3
DIFF_HEADS = 4
DIFF_HEAD_DIM = GROUP_WIDTH // (2 * DIFF_HEADS)
MEM_HEADS = 4
MEM_HEAD_DIM = GROUP_WIDTH // MEM_HEADS
N_MEM = 256
Q_BLOCK = 128
PEER_HEADS = 8
PEER_N_KEYS = 128
PEER_N_EXPERTS = PEER_N_KEYS * PEER_N_KEYS
PEER_TOPK = 16
PEER_QUERY_DIM = 256
PEER_HALF = PEER_QUERY_DIM // 2
PEER_CHUNK = 128
EPS = 1e-6
IN_COLS = 10 * GROUP_WIDTH
IN_SPLITS = (GROUP_WIDTH, 3 * GROUP_WIDTH, 6 * GROUP_WIDTH, 9 * GROUP_WIDTH)

kernel_name = 'hybrid_parallel_groups_peer_encoder'


def rmsnorm(x, g):
    xf = x.astype(jnp.float32)
    xf = xf * lax.rsqrt(jnp.mean(xf * xf, axis=-1, keepdims=True) + EPS)
    return xf.astype(x.dtype) * g


def layernorm(x, g, b):
    xf = x.astype(jnp.float32)
    mu = jnp.mean(xf, axis=-1, keepdims=True)
    var = jnp.mean(jnp.square(xf - mu), axis=-1, keepdims=True)
    return ((xf - mu) * lax.rsqrt(var + EPS)).astype(x.dtype) * g + b


def depthwise_conv(x, w):
    k = w.shape[0]
    pad = k // 2
    return lax.conv_general_dilated(x, w[:, None, :].astype(x.dtype), (1,), [(pad, pad)],
                                    dimension_numbers=('NWC', 'WIO', 'NWC'),
                                    feature_group_count=x.shape[-1])


def alibi_slopes(n):
    return 2.0 ** (-8.0 * jnp.arange(1, n + 1, dtype=jnp.float32) / n)


def pool_mixer(u, w, scale):
    b, s, _ = u.shape
    cs = jnp.concatenate([jnp.zeros((b, 1, GROUP_WIDTH), jnp.float32),
                          jnp.cumsum(u.astype(jnp.float32), axis=1)], axis=1)
    t = jnp.arange(s)
    outs = []
    for gi, win in enumerate(POOL_WINDOWS):
        lo = jnp.clip(t - win // 2, 0, s)
        hi = jnp.clip(t + win // 2, 0, s)
        csg = cs[:, :, gi * POOL_CH:(gi + 1) * POOL_CH]
        outs.append((csg[:, hi] - csg[:, lo]) / (hi - lo).astype(jnp.float32)[None, :, None])
    pooled = jnp.concatenate(outs, axis=-1).astype(u.dtype) - u
    y = jnp.einsum('bsgc,gcd->bsgd', pooled.reshape(b, s, len(POOL_WINDOWS), POOL_CH), w)
    return y.reshape(b, s, GROUP_WIDTH) * scale


def conformer_conv(u, dw, ln_g, ln_b, pw):
    a, gate = jnp.split(u, 2, axis=-1)
    h = a * jax.nn.sigmoid(gate)
    h = depthwise_conv(h, dw)
    h = jax.nn.silu(layernorm(h, ln_g, ln_b))
    return h @ pw


def diff_attention(u, qk_g, lam_p, subln_g, layer_idx):
    b, s, _ = u.shape
    q, k, v = jnp.split(u, 3, axis=-1)
    q = rmsnorm(q.reshape(b, s, 2, DIFF_HEADS, DIFF_HEAD_DIM), qk_g[0])
    k = rmsnorm(k.reshape(b, s, 2, DIFF_HEADS, DIFF_HEAD_DIM), qk_g[1])
    v = v.reshape(b, s, DIFF_HEADS, 2 * DIFF_HEAD_DIM)
    lam_init = 0.8 - 0.6 * math.exp(-0.3 * layer_idx)
    lp = lam_p.astype(jnp.float32)
    lam = jnp.exp(jnp.sum(lp[0] * lp[1])) - jnp.exp(jnp.sum(lp[2] * lp[3])) + lam_init
    slopes = alibi_slopes(DIFF_HEADS)
    scale = DIFF_HEAD_DIM ** -0.5
    nb = s // Q_BLOCK
    qb = q.reshape(b, nb, Q_BLOCK, 2, DIFF_HEADS, DIFF_HEAD_DIM).transpose(1, 0, 2, 3, 4, 5)
    starts = jnp.arange(nb, dtype=jnp.int32) * Q_BLOCK
    kpos = jnp.arange(s, dtype=jnp.int32)

    def block(args):
        qblk, start = args
        sc = jnp.einsum('bqjhd,bkjhd->bjhqk', qblk, k).astype(jnp.float32) * scale
        qpos = start + jnp.arange(Q_BLOCK, dtype=jnp.int32)
        dist = jnp.abs(qpos[:, None] - kpos[None, :]).astype(jnp.float32)
        sc = sc - slopes[:, None, None] * dist
        p = jax.nn.softmax(sc, axis=-1)
        a = p[:, 0] - lam * p[:, 1]
        return jnp.einsum('bhqk,bkhe->bqhe', a.astype(v.dtype), v)

    o = lax.map(block, (qb, starts))
    o = o.transpose(1, 0, 2, 3, 4).reshape(b, s, DIFF_HEADS, 2 * DIFF_HEAD_DIM)
    o = rmsnorm(o, subln_g) * (1.0 - lam_init)
    return o.reshape(b, s, GROUP_WIDTH)


def short_conv(u, w):
    h, bg, cg = jnp.split(u, 3, axis=-1)
    return bg * depthwise_conv(cg * h, w)


def memory_attention(u, mem, mem_g, w_kv, qk_g):
    b, s, _ = u.shape
    m = mem.shape[1]
    kv = rmsnorm(mem, mem_g) @ w_kv
    k, v = jnp.split(kv, 2, axis=-1)
    q = rmsnorm(u.reshape(b, s, MEM_HEADS, MEM_HEAD_DIM), qk_g[0])
    k = rmsnorm(k.reshape(b, m, MEM_HEADS, MEM_HEAD_DIM), qk_g[1])
    v = v.reshape(b, m, MEM_HEADS, MEM_HEAD_DIM)
    sc = jnp.einsum('bshd,bmhd->bhsm', q, k).astype(jnp.float32) * (MEM_HEAD_DIM ** -0.5)
    p = jax.nn.softmax(sc, axis=-1).astype(v.dtype)
    return jnp.einsum('bhsm,bmhe->bshe', p, v).reshape(b, s, GROUP_WIDTH)


def peer_ffn(xn, wq, subkeys, u_tab, v_tab):
    b, s, d = xn.shape
    q = (xn @ wq).reshape(b, s, PEER_HEADS, 2, PEER_HALF)
    sc = jnp.einsum('bshjc,hjkc->bshjk', q, subkeys).astype(jnp.float32)
    top_s, top_i = lax.top_k(sc, PEER_TOPK)
    cand = (top_s[..., 0, :, None] + top_s[..., 1, None, :]).reshape(b, s, PEER_HEADS, PEER_TOPK * PEER_TOPK)
    best, ci = lax.top_k(cand, PEER_TOPK)
    i1 = jnp.take_along_axis(top_i[..., 0, :], ci // PEER_TOPK, axis=-1)
    i2 = jnp.take_along_axis(top_i[..., 1, :], ci % PEER_TOPK, axis=-1)
    idx = i1 * PEER_N_KEYS + i2
    g = jax.nn.softmax(best, axis=-1)
    t = b * s
    nc = t // PEER_CHUNK
    idx_c = idx.reshape(nc, PEER_CHUNK, PEER_HEADS * PEER_TOPK)
    g_c = g.reshape(nc, PEER_CHUNK, PEER_HEADS * PEER_TOPK)
    x_c = xn.reshape(nc, PEER_CHUNK, d)

    def chunk(args):
        ic, gc, xc = args
        act = jax.nn.gelu(jnp.einsum('ckd,cd->ck', u_tab[ic], xc), approximate=False)
        return jnp.einsum('ck,ckd->cd', (gc * act).astype(v_tab.dtype), v_tab[ic])

    return lax.map(chunk, (idx_c, g_c, x_c)).reshape(b, s, d)


def setup_inputs(seed: int = 0) -> dict:
    key = jax.random.key(seed)
    ks = jax.random.split(key, 24)
    f32 = jnp.float32
    nrm = lambda k, shape, sc: jax.random.normal(k, shape, f32) * sc
    gain = lambda k, shape: 1.0 + 0.02 * jax.random.normal(k, shape, f32)
    L = DEPTH
    return {
        'x': nrm(ks[0], (BATCH, SEQ, D_MODEL), 1.0),
        'mem': nrm(ks[1], (BATCH, N_MEM, D_MODEL), 1.0),
        'norm_mix': gain(ks[2], (L, D_MODEL)),
        'w_in': nrm(ks[3], (L, D_MODEL, IN_COLS), D_MODEL ** -0.5),
        'pool_w': nrm(ks[4], (L, len(POOL_WINDOWS), POOL_CH, POOL_CH), POOL_CH ** -0.5),
        'pool_scale': 1.0 + 0.1 * jax.random.normal(ks[5], (L, GROUP_WIDTH), f32),
        'conv_dw': nrm(ks[6], (L, CONV_WIDTH, GROUP_WIDTH), CONV_WIDTH ** -0.5),
        'conv_ln_g': gain(ks[7], (L, GROUP_WIDTH)),
        'conv_ln_b': nrm(ks[8], (L, GROUP_WIDTH), 0.02),
        'conv_pw': nrm(ks[9], (L, GROUP_WIDTH, GROUP_WIDTH), GROUP_WIDTH ** -0.5),
        'diff_qk_norm': gain(ks[10], (L, 2, DIFF_HEAD_DIM)),
        'diff_lambda': nrm(ks[11], (L, 4, DIFF_HEAD_DIM), 0.1),
        'diff_subln': gain(ks[12], (L, 2 * DIFF_HEAD_DIM)),
        'sconv_w': nrm(ks[13], (L, SHORT_CONV_WIDTH, GROUP_WIDTH), SHORT_CONV_WIDTH ** -0.5),
        'mem_norm': gain(ks[14], (L, D_MODEL)),
        'w_mem_kv': nrm(ks[15], (L, D_MODEL, 2 * GROUP_WIDTH), D_MODEL ** -0.5),
        'mem_qk_norm': gain(ks[16], (L, 2, MEM_HEAD_DIM)),
        'group_norm': gain(ks[17], (L, MIX_WIDTH)),
        'w_out': nrm(ks[18], (L, MIX_WIDTH, D_MODEL), MIX_WIDTH ** -0.5),
        'norm_ffn': gain(ks[19], (L, D_MODEL)),
        'peer_wq': nrm(ks[20], (L, D_MODEL, PEER_HEADS * PEER_QUERY_DIM), D_MODEL ** -0.5),
        'peer_subkeys': nrm(ks[21], (L, PEER_HEADS, 2, PEER_N_KEYS, PEER_HALF), PEER_HALF ** -0.5),
        'peer_u': nrm(ks[22], (L, PEER_N_EXPERTS, D_MODEL), D_MODEL ** -0.5),
        'peer_v': nrm(ks[23], (L, PEER_N_EXPERTS, D_MODEL), 0.1),
    }


def reference(x, mem, norm_mix, w_in, pool_w, pool_scale, conv_dw, conv_ln_g, conv_ln_b, conv_pw,
              diff_qk_norm, diff_lambda, diff_subln, sconv_w, mem_norm, w_mem_kv, mem_qk_norm,
              group_norm, w_out, norm_ffn, peer_wq, peer_subkeys, peer_u, peer_v):
    b, s, _ = x.shape
    for l in range(DEPTH):
        xn = rmsnorm(x, norm_mix[l])
        proj = xn @ w_in[l]
        u_pool, u_conv, u_diff, u_sc, u_mem = jnp.split(proj, IN_SPLITS, axis=-1)
        y_pool = pool_mixer(u_pool, pool_w[l], pool_scale[l])
        y_conv = conformer_conv(u_conv, conv_dw[l], conv_ln_g[l], conv_ln_b[l], conv_pw[l])
        y_diff = diff_attention(u_diff, diff_qk_norm[l], diff_lambda[l], diff_subln[l], l)
        y_sc = short_conv(u_sc, sconv_w[l])
        y_mem = memory_attention(u_mem, mem, mem_norm[l], w_mem_kv[l], mem_qk_norm[l])
        cat = jnp.stack([y_pool, y_conv, y_diff, y_sc, y_mem], axis=2)
        cat = rmsnorm(cat, group_norm[l].reshape(N_GROUPS, GROUP_WIDTH)).reshape(b, s, MIX_WIDTH)
        x = x + cat @ w_out[l]
        x = x + peer_ffn(rmsnorm(x, norm_ffn[l]), peer_wq[l], peer_subkeys[l], peer_u[l], peer_v[l])
    return x
```

```python
import math
import numpy as np
import ml_dtypes
import concourse.bass as bass
import concourse.mybir as mybir
from concourse.bass_utils import run_bass_kernel_spmd

F32 = mybir.dt.float32
BF16 = mybir.dt.bfloat16
I32 = mybir.dt.int32
U32 = mybir.dt.uint32
AF = mybir.ActivationFunctionType
ALU = mybir.AluOpType
AX = mybir.AxisListType

D = 1024
GW = 256
EPS = 1e-6
NMEM = 256
SEQ = 4096
DEPTH = 4
NKEYS = 128
NEXP = 16384


class V:
    def __init__(self, ap, t, key=None):
        self.ap = ap
        self.t = t
        self.key = key

    def __getitem__(self, idx):
        return V(self.ap[idx], self.t, self.key)

    def rr(self, pat, **kw):
        return V(self.ap.rearrange(pat, **kw), self.t, self.key)

    def bc(self, shape):
        return V(self.ap.broadcast_to(list(shape)), self.t, self.key)

    def m(self, f):
        return V(f(self.ap), self.t, self.key)


class T:
    def __init__(self, root, name, is_ap=False):
        self.root = root
        self.name = name
        self.st = {}

    def __getitem__(self, idx):
        return V(self.root[idx], self, None)

    def k(self, key):
        t = self

        class _K:
            def __getitem__(self, idx):
                return V(t.root[idx], t, key)

        return _K()


class Eng:
    def __init__(self, fw, name, hw):
        self.fw = fw
        self.name = name
        self.hw = hw
        self.waited = {}
        self.sem = fw.new_sem(name)
        self.count = 0

    def wait(self, ev):
        if ev is None:
            return
        sem, val = ev
        k = id(sem)
        if sem is self.sem and (val > self.count or self.name == "pe"):
            return
        if self.waited.get(k, 0) >= val:
            return
        self.hw.wait_ge(sem, val)
        self.waited[k] = val
        self.fw.n_waits += 1


class FW:
    def __init__(self, nc, n_dma_sems=32):
        self.nc = nc
        self._stack = []
        self._scopes = []
        self.n_waits = 0
        self.n_inst = 0
        self.tiles = []
        self.pe = Eng(self, "pe", nc.tensor)
        self.act = Eng(self, "act", nc.scalar)
        self.dve = Eng(self, "dve", nc.vector)
        self.pool = Eng(self, "pool", nc.gpsimd)
        self.sp = Eng(self, "sp", nc.sync)
        self.engs = [self.pe, self.act, self.dve, self.pool, self.sp]
        self.dma_pools = {
            "sp": [[[self.new_sem(f"dsp{i}"), 0] for i in range(24)], 0],
            "act": [[[self.new_sem(f"dac{i}"), 0] for i in range(8)], 0],
        }

    def new_sem(self, name):
        cm = self.nc.semaphore(name)
        s = cm.__enter__()
        self._stack.append(cm)
        return s

    def _alloc(self, cm, name):
        t = cm.__enter__()
        if self._scopes:
            self._scopes[-1].append(cm)
        else:
            self._stack.append(cm)
        tt = T(t, name)
        self.tiles.append(tt)
        return tt

    def _uname(self, name):
        self.uid = getattr(self, "uid", 0) + 1
        return f"{name}_{self.uid}"

    def sbuf(self, name, shape, dtype):
        name = self._uname(name)
        return self._alloc(self.nc.sbuf_tensor(name, list(shape), dtype), name)

    def psum(self, name, shape, dtype=F32):
        name = self._uname(name)
        return self._alloc(self.nc.psum_tensor(name, list(shape), dtype), name)

    def dram(self, name, shape, dtype, kind="Internal"):
        t = self.nc.dram_tensor(name, list(shape), dtype, kind=kind)
        tt = T(t.ap(), name)
        self.tiles.append(tt)
        return tt

    def push(self):
        self._scopes.append([])

    def pop(self):
        self.barrier()
        for cm in reversed(self._scopes.pop()):
            cm.__exit__(None, None, None)

    def close(self):
        while self._stack:
            self._stack.pop().__exit__(None, None, None)

    def barrier(self):
        evs = [(e.sem, e.count) for e in self.engs if e.count > 0]
        for pl, _ in self.dma_pools.values():
            evs += [(s, v) for s, v in pl if v > 0]
        for e in self.engs:
            for ev in evs:
                e.wait(ev)
        for t in self.tiles:
            t.st = {}

    def _deps(self, eng, reads, writes):
        for v in reads:
            for k, st in v.t.st.items():
                if v.key is None or k is None or k == v.key:
                    eng.wait(st[0])
        for v in writes:
            for k, st in v.t.st.items():
                if v.key is None or k is None or k == v.key:
                    eng.wait(st[0])
                    for ev in st[1].values():
                        eng.wait(ev)

    def _mark(self, ev, reads, writes):
        sem, val = ev
        for v in reads:
            st = v.t.st.setdefault(v.key, [None, {}])
            st[1][id(sem)] = ev
        for v in writes:
            if v.key is None:
                v.t.st = {None: [ev, {}]}
            else:
                v.t.st[v.key] = [ev, {}]

    def op(self, eng, fn, reads=(), writes=(), signal=True):
        reads = [r for r in reads if isinstance(r, V)]
        self._deps(eng, reads, writes)
        ins = fn()
        self.n_inst += 1
        if signal:
            ins.then_inc(eng.sem, 1)
            eng.count += 1
            ev = (eng.sem, eng.count)
        else:
            ev = (eng.sem, eng.count + 1)
        self._mark(ev, reads, writes)
        return ev

    def dma(self, out, in_, q=None, **kw):
        q = q or self.sp
        self._deps(q, [in_], [out])
        pl = self.dma_pools[q.name]
        slot = pl[0][pl[1]]
        pl[1] = (pl[1] + 1) % len(pl[0])
        sem, val = slot
        if val > 0:
            q.wait((sem, val))
        ins = q.hw.dma_start(out=out.ap, in_=in_.ap, **kw)
        ins.then_inc(sem, 16)
        slot[1] = val + 16
        ev = (sem, val + 16)
        self.n_inst += 1
        self._mark(ev, [in_], [out])
        return ev

    @staticmethod
    def _a(x):
        return x.ap if isinstance(x, V) else x

    def mm(self, out, lhsT, rhs, start=True, stop=True, signal=None):
        if signal is None:
            signal = stop
        return self.op(self.pe, lambda: self.nc.tensor.matmul(out.ap, lhsT=lhsT.ap, rhs=rhs.ap, start=start, stop=stop),
                       [lhsT, rhs], [out], signal=signal)

    def tr(self, out, in_, ident, signal=True):
        return self.op(self.pe, lambda: self.nc.tensor.transpose(out.ap, in_.ap, ident.ap), [in_, ident], [out], signal=signal)

    def actv(self, out, in_, func, bias=None, scale=None, accum=None):
        kw = {}
        if bias is not None:
            kw["bias"] = self._a(bias)
        if scale is not None:
            kw["scale"] = self._a(scale)
        w = [out]
        if accum is not None:
            kw["accum_out"] = accum.ap
            w.append(accum)
        return self.op(self.act, lambda: self.nc.scalar.activation(out.ap, in_.ap, func, **kw), [in_, bias, scale], w)

    def _ve(self, eng):
        return self.nc.vector if eng is self.dve else self.nc.gpsimd

    def ts(self, eng, out, in0, s1, s2, op0, op1=None, accum=None):
        kw = {}
        if op1 is not None:
            kw["op1"] = op1
        w = [out]
        if accum is not None:
            kw["accum_out"] = accum.ap
            w.append(accum)
        return self.op(eng, lambda: self._ve(eng).tensor_scalar(out.ap, in0.ap, self._a(s1), self._a(s2), op0, **kw),
                       [in0, s1, s2], w)

    def tt(self, eng, out, in0, in1, op):
        return self.op(eng, lambda: self._ve(eng).tensor_tensor(out.ap, in0.ap, in1.ap, op), [in0, in1], [out])

    def stt(self, out, in0, scalar, in1, op0, op1):
        return self.op(self.dve, lambda: self.nc.vector.scalar_tensor_tensor(out.ap, in0.ap, self._a(scalar), in1.ap, op0, op1),
                       [in0, scalar, in1], [out])

    def copy(self, eng, out, in_):
        if eng is self.act:
            return self.op(eng, lambda: self.nc.scalar.copy(out.ap, in_.ap), [in_], [out])
        return self.op(eng, lambda: self._ve(eng).tensor_copy(out.ap, in_.ap), [in_], [out])

    def recip(self, out, in_):
        return self.op(self.dve, lambda: self.nc.vector.reciprocal(out.ap, in_.ap), [in_], [out])

    def reduce(self, out, in_, op, axis=AX.X):
        return self.op(self.dve, lambda: self.nc.vector.tensor_reduce(out.ap, in_.ap, axis, op), [in_], [out])

    def memset(self, eng, out, val):
        return self.op(eng, lambda: self._ve(eng).memset(out.ap, val), [], [out])

    def rsqrt(self, out, in_, scale, eps):
        self.actv(out, in_, AF.Sqrt, bias=eps, scale=scale)
        self.recip(out, out)


def make_consts(S):
    bf = ml_dtypes.bfloat16
    pos = np.arange(S)
    hi = (pos // 64).astype(np.float64)
    lo = (pos % 64).astype(np.float64)
    slopes = [2.0 ** (-8.0 * (h + 1) / 4) for h in range(4)]
    aq = np.zeros((4, 4, S), np.float64)
    ak = np.zeros((4, 4, S), np.float64)
    dmat = np.zeros((4, 128, 128), np.float64)
    dd = np.abs(np.arange(128)[:, None] - np.arange(128)[None, :])
    for h, sl in enumerate(slopes):
        aq[h, 0] = 64 * sl * hi
        aq[h, 1] = sl * lo
        aq[h, 2] = 1
        aq[h, 3] = 1
        ak[h, 0] = -1
        ak[h, 1] = -1
        ak[h, 2] = 64 * sl * hi
        ak[h, 3] = sl * lo
        dmat[h] = -sl * dd
    edge = np.ones((2, 128, 16), np.float64)
    for g, w in enumerate((2, 4, 8, 16)):
        tl, pr = g // 2, (g % 2) * 64
        for i in range(8):
            t = i
            cnt = min(t + w // 2, S) - max(t - w // 2, 0)
            edge[tl, pr:pr + 64, i] = w / cnt
            t = S - 8 + i
            cnt = min(t + w // 2, S) - max(t - w // 2, 0)
            edge[tl, pr:pr + 64, 8 + i] = w / cnt
    invw = np.zeros((2, 128, 1), np.float64)
    for g, w in enumerate((2, 4, 8, 16)):
        invw[g // 2, (g % 2) * 64:(g % 2) * 64 + 64, 0] = 1.0 / w
    p = np.arange(128)
    blk32 = (p[:, None] // 32 == p[None, :] // 32).astype(np.float64)
    blk64 = (p[:, None] // 64 == p[None, :] // 64).astype(np.float64)
    return {
        "c_aq": aq.astype(bf), "c_aqn": (-aq).astype(bf), "c_ak": ak.astype(bf),
        "c_dmat": dmat.transpose(1, 0, 2).copy().astype(bf),
        "c_edge": edge.transpose(1, 0, 2).copy().astype(np.float32),
        "c_invw": invw.transpose(1, 0, 2).copy().astype(np.float32),
        "c_blk32": blk32.astype(bf), "c_blk64": blk64.astype(bf),
        "c_ident": np.eye(128).astype(bf), "c_identf": np.eye(128).astype(np.float32),
        "c_iota": np.tile(np.arange(128, dtype=np.float32)[None, :], (128, 1)),
    }


PARAM_SHAPES = {
    "norm_mix": (D,), "w_in": (D, 2560), "pool_w": (4, 64, 64), "pool_scale": (GW,), "conv_dw": (31, GW),
    "conv_ln_g": (GW,), "conv_ln_b": (GW,), "conv_pw": (GW, GW), "diff_qk_norm": (2, 32), "diff_lambda": (4, 32),
    "diff_subln": (64,), "sconv_w": (3, GW), "mem_norm": (D,), "w_mem_kv": (D, 512), "mem_qk_norm": (2, 64),
    "group_norm": (1280,), "w_out": (1280, D), "norm_ffn": (D,), "peer_wq": (D, 2048),
    "peer_subkeys": (8, 2, 128, 128), "peer_u": (NEXP, D), "peer_v": (NEXP, D),
}


def build(S=SEQ, depth=DEPTH, with_peer=True):
    nc = bass.Bass("TRN2", target_bir_lowering=False)
    fw = FW(nc)
    NCH = S // 512
    NKT = S // 128
    cs = make_consts(S)

    def ext(name, shape, dt=F32):
        return T(nc.dram_tensor(name, list(shape), dt, kind="ExternalInput").ap(), name)

    x_in = ext("x", (S, D))
    mem_in = ext("mem", (NMEM, D))
    P = {k: ext(k, (depth,) + shp) for k, shp in PARAM_SHAPES.items()}
    C = {k: ext(k, v.shape, BF16 if v.dtype == ml_dtypes.bfloat16 else F32) for k, v in cs.items()}
    out_d = T(nc.dram_tensor("out", [S, D], F32, kind="ExternalOutput").ap(), "out")

    xbuf = fw.dram("xbuf", (S, D), F32)
    xmid = fw.dram("xmid", (S, D), F32)
    projT = fw.dram("projT", (14, 128, S), F32)
    qTd = fw.dram("qTd", (4, 128, S), BF16)
    qmTd = fw.dram("qmTd", (2, 128, S), BF16)
    uT_d = fw.dram("uT_d", (128, 128, 8, 128), BF16)
    v_d = fw.dram("v_d", (NEXP, D), BF16)

    dve, act, pool, pe = fw.dve, fw.act, fw.pool, fw.pe
    stq = fw.act

    ident = fw.sbuf("ident", [128, 128], BF16)
    identf = fw.sbuf("identf", [128, 128], F32)
    ones = fw.sbuf("ones", [128, 128], BF16)
    blk32 = fw.sbuf("blk32", [128, 128], BF16)
    blk64 = fw.sbuf("blk64", [128, 128], BF16)
    dmat = fw.sbuf("dmat", [128, 4, 128], BF16)
    edge = fw.sbuf("edge", [128, 2, 16], F32)
    invw = fw.sbuf("invw", [128, 2, 1], F32)
    iota = fw.sbuf("iota", [128, 128], F32)
    fw.dma(ident[:], C["c_ident"][:])
    fw.dma(identf[:], C["c_identf"][:])
    fw.dma(blk32[:], C["c_blk32"][:])
    fw.dma(blk64[:], C["c_blk64"][:])
    fw.dma(dmat[:], C["c_dmat"][:])
    fw.dma(edge[:], C["c_edge"][:])
    fw.dma(invw[:], C["c_invw"][:])
    fw.dma(iota[:], C["c_iota"][:])
    fw.memset(dve, ones[:], 1.0)
    onesf = fw.sbuf("onesf", [128, 64], F32)
    fw.memset(dve, onesf[:], 1.0)
    ldstg = fw.sbuf("ldstg", [32, 128], F32)

    def load_T(dst, src, n, ps):
        fw.dma(ldstg[0:n, :], src)
        fw.tr(ps[:, 0:n], ldstg[0:n, :], identf[0:n, 0:n])
        fw.copy(dve, dst, ps[:, 0:n])
    cur_x = x_in

    for l in range(depth):
        last = l == depth - 1
        lam_init = 0.8 - 0.6 * math.exp(-0.3 * l)
        dst_x = out_d if last else xbuf

        fw.push()
        kT = [fw.sbuf(f"kT{i}", [128, S], BF16) for i in range(4)]
        Vaug = fw.sbuf("Vaug", [128, NKT, 4, 66], BF16)
        kmT = [fw.sbuf(f"kmT{i}", [128, NMEM], BF16) for i in range(2)]
        Vm = fw.sbuf("Vm", [128, 2, 4, 66], BF16)
        for i in range(4):
            fw.memset(pool, kT[i][:], 0.0)
        fw.memset(dve, Vaug[:], 1.0)
        fw.memset(dve, Vm[:], 1.0)
        for hp in range(4):
            for side in range(2):
                h = 2 * (hp % 2) + side
                fw.dma(kT[hp][side * 64 + 32:side * 64 + 36, :], C["c_ak"][h])

        fw.push()
        W = fw.sbuf("W", [128, 8, 3072], BF16)
        stg = [fw.sbuf(f"stg{i}", [128, 2560], F32) for i in range(2)]
        gmix = fw.sbuf("gmix", [128, 8], F32)
        gmem = fw.sbuf("gmem", [128, 8], F32)
        gq = fw.sbuf("gq", [128, 1], F32)
        gk = fw.sbuf("gk", [128, 1], F32)
        gmq = fw.sbuf("gmq", [128, 1], F32)
        gmk = fw.sbuf("gmk", [128, 1], F32)
        Wkv = fw.sbuf("Wkv", [128, 8, 512], BF16)
        memnT = fw.sbuf("memnT", [128, 8, NMEM], BF16)
        xt = [fw.sbuf(f"xt{i}", [128, D], F32) for i in range(2)]
        xs = [fw.sbuf(f"xs{i}", [128, D], BF16) for i in range(2)]
        junk = fw.sbuf("junk", [128, D], F32)
        ssv = [fw.sbuf(f"ssv{i}", [128, 1], F32) for i in range(2)]
        xnT = [fw.sbuf(f"xnT{i}", [128, 8, 512], BF16) for i in range(2)]
        ev_f = [fw.sbuf(f"evf{i}", [128, 512], F32) for i in range(3)]
        sqb = [fw.sbuf(f"sqb{i}", [128, 512], BF16) for i in range(2)]
        rst = [fw.sbuf(f"rst{i}", [128, 512], F32) for i in range(2)]
        qn = [fw.sbuf(f"qn{i}", [128, 512], BF16) for i in range(2)]
        tp = [fw.psum(f"tp{i}", [128, 8, 128], BF16) for i in range(2)]
        pj = [fw.psum(f"pj{i}", [128, 512], F32) for i in range(3)]
        pss = [fw.psum(f"pss{i}", [128, 512], F32) for i in range(2)]

        fw.memset(pool, W[:], 0.0)
        fw.memset(dve, gq[:], 0.0)
        fw.memset(dve, gk[:], 0.0)
        load_T(gmix[:], P["norm_mix"][l].rr("(kc kp) -> kc kp", kp=128), 8, pj[0])
        load_T(gmem[:], P["mem_norm"][l].rr("(kc kp) -> kc kp", kp=128), 8, pj[1])
        for side in range(2):
            fw.dma(gq[side * 64:side * 64 + 32, :], P["diff_qk_norm"][l, 0].rr("(d o) -> d o", o=1))
            fw.dma(gk[side * 64:side * 64 + 32, :], P["diff_qk_norm"][l, 1].rr("(d o) -> d o", o=1))
            fw.dma(gmq[side * 64:side * 64 + 64, :], P["mem_qk_norm"][l, 0].rr("(d o) -> d o", o=1))
            fw.dma(gmk[side * 64:side * 64 + 64, :], P["mem_qk_norm"][l, 1].rr("(d o) -> d o", o=1))
        fw.ts(dve, gq[:], gq[:], 32 ** -0.5, None, ALU.mult)
        fw.ts(dve, gmq[:], gmq[:], 64 ** -0.5, None, ALU.mult)

        for kc in range(8):
            sg = stg[kc % 2]
            fw.dma(sg[:], P["w_in"][l, kc * 128:(kc + 1) * 128, :])
            e = pool if kc % 2 == 0 else act
            fw.copy(e, W[:, kc, 0:768], sg[:, 0:768])
            for off_in, off_out in ((768, 768), (1024, 1280)):
                fw.copy(e, W[:, kc, off_out:off_out + 512].rr("p (a b c) -> p a b c", a=4, b=2)[:, :, :, 0:32],
                        sg[:, off_in:off_in + 256].rr("p (a b c) -> p a b c", a=4, b=2))
            fw.copy(e, W[:, kc, 1792:3072], sg[:, 1280:2560])
        for kc in range(8):
            sg = stg[kc % 2]
            fw.dma(sg[:, 0:512], P["w_mem_kv"][l, kc * 128:(kc + 1) * 128, :])
            fw.copy(pool if kc % 2 == 0 else act, Wkv[:, kc, :], sg[:, 0:512])

        def norm_T(src_v, i, gcol, dstT, col0):
            fw.dma(xt[i][:], src_v)
            fw.actv(junk[:], xt[i][:], AF.Square, accum=ssv[i][:])
            fw.rsqrt(ssv[i][:], ssv[i][:], 1.0 / D, EPS)
            fw.actv(xs[i][:], xt[i][:], AF.Copy, scale=ssv[i][:])
            for kc in range(8):
                fw.tr(tp[i][:, kc, :], xs[i][:, kc * 128:(kc + 1) * 128], ident[:], signal=(kc == 7))
            fw.tt(dve, dstT[:, :, col0:col0 + 128], tp[i][:], gcol[:].m(lambda a: a.unsqueeze(2).broadcast_to([128, 8, 128])), ALU.mult)

        def qk_norm(ps, blk, gcol, nh, i):
            fw.actv(sqb[i][:], ps[:], AF.Square)
            fw.mm(pss[i][:], blk[:], sqb[i][:])
            fw.rsqrt(rst[i][:], pss[i][:], 1.0 / nh, EPS)

        for mt in range(2):
            norm_T(mem_in[mt * 128:(mt + 1) * 128, :], mt, gmem, memnT, mt * 128)
        for hp in range(2):
            ps = pj[hp]
            for kc in range(8):
                fw.mm(ps[:, 0:NMEM], Wkv[:, kc, hp * 128:(hp + 1) * 128], memnT[:, kc, :], start=(kc == 0), stop=(kc == 7))
            fw.actv(sqb[hp][:, 0:NMEM], ps[:, 0:NMEM], AF.Square)
            fw.mm(pss[hp][:, 0:NMEM], blk64[:], sqb[hp][:, 0:NMEM])
            fw.rsqrt(rst[hp][:, 0:NMEM], pss[hp][:, 0:NMEM], 1.0 / 64, EPS)
            fw.stt(kmT[hp][:], ps[:, 0:NMEM], gmk[:], rst[hp][:, 0:NMEM], ALU.mult, ALU.mult)
        for mt in range(2):
            ps = pj[2]
            for kc in range(8):
                fw.mm(ps[:, 0:256], memnT[:, kc, mt * 128:(mt + 1) * 128], Wkv[:, kc, 256:512], start=(kc == 0), stop=(kc == 7))
            fw.copy(act, Vm[:, mt, :, 0:64], ps[:, 0:256].rr("p (h e) -> p h e", h=4))

        conv_chunks = [0, 128, 256, 384, 512, 640] + [2048 + 128 * i for i in range(6)]
        for c in range(NCH):
            xT = xnT[c % 2]
            for sub in range(4):
                r0 = c * 512 + sub * 128
                norm_T(cur_x[r0:r0 + 128, :], sub % 2, gmix, xT, sub * 128)
            cnt = 0
            for ci, wo in enumerate(conv_chunks):
                ps = pj[cnt % 3]
                evb = ev_f[cnt % 3]
                cnt += 1
                for kc in range(8):
                    fw.mm(ps[:], W[:, kc, wo:wo + 128], xT[:, kc, :], start=(kc == 0), stop=(kc == 7))
                fw.copy(act, evb[:], ps[:])
                fw.dma(projT.k(("p", ci, c))[ci, :, c * 512:(c + 1) * 512], evb[:], q=stq)
            for hp in range(4):
                ps = pj[cnt % 3]
                cnt += 1
                i = hp % 2
                for kc in range(8):
                    fw.mm(ps[:], W[:, kc, 768 + hp * 128:768 + (hp + 1) * 128], xT[:, kc, :], start=(kc == 0), stop=(kc == 7))
                qk_norm(ps, blk32, gq, 32, i)
                fw.stt(qn[i][:], ps[:], gq[:], rst[i][:], ALU.mult, ALU.mult)
                fw.dma(qTd.k(("q", hp, c))[hp, :, c * 512:(c + 1) * 512], qn[i][:], q=stq)
            for hp in range(4):
                ps = pj[cnt % 3]
                cnt += 1
                i = hp % 2
                for kc in range(8):
                    fw.mm(ps[:], W[:, kc, 1280 + hp * 128:1280 + (hp + 1) * 128], xT[:, kc, :], start=(kc == 0), stop=(kc == 7))
                qk_norm(ps, blk32, gk, 32, i)
                for side in range(2):
                    r = slice(side * 64, side * 64 + 32)
                    fw.stt(kT[hp].k(("k", c))[r, c * 512:(c + 1) * 512], ps[r, :], gk[r, :], rst[i][r, :], ALU.mult, ALU.mult)
            for hp in range(2):
                ps = pj[cnt % 3]
                cnt += 1
                i = hp % 2
                for kc in range(8):
                    fw.mm(ps[:], W[:, kc, 2816 + hp * 128:2816 + (hp + 1) * 128], xT[:, kc, :], start=(kc == 0), stop=(kc == 7))
                qk_norm(ps, blk64, gmq, 64, i)
                fw.stt(qn[i][:], ps[:], gmq[:], rst[i][:], ALU.mult, ALU.mult)
                fw.dma(qmTd.k(("q", hp, c))[hp, :, c * 512:(c + 1) * 512], qn[i][:], q=stq)
            for sub in range(4):
                ps = pj[cnt % 3]
                cnt += 1
                for kc in range(8):
                    fw.mm(ps[:, 0:256], xT[:, kc, sub * 128:(sub + 1) * 128], W[:, kc, 1792:2048], start=(kc == 0), stop=(kc == 7))
                fw.copy(act, Vaug.k(("v", c, sub))[:, c * 4 + sub, :, 0:64], ps[:, 0:256].rr("p (h e) -> p h e", h=4))
        fw.pop()

        fw.push()
        Wo = fw.sbuf("Wo", [128, 14, D], BF16)
        stgo = [fw.sbuf(f"stgo{i}", [128, D], F32) for i in range(2)]
        gn = fw.sbuf("gn", [128, 14], F32)
        BD = fw.sbuf("BD", [128, 2, 128], BF16)
        BDf = fw.sbuf("BDf", [128, 2, 128], F32)
        pscale = fw.sbuf("pscale", [128, 2], F32)
        cdw = fw.sbuf("cdw", [128, 2, 31], F32)
        diag = fw.sbuf("diag", [128, 2, 31, 128], BF16)
        lng = fw.sbuf("lng", [128, 2], F32)
        lnb = fw.sbuf("lnb", [128, 2], F32)
        pw = fw.sbuf("pw", [128, 2, 256], BF16)
        scw = fw.sbuf("scw", [128, 2, 3], F32)
        lamt = fw.sbuf("lamt", [64, 4, 32], F32)
        lamp = fw.sbuf("lamp", [64, 2, 32], F32)
        lams = fw.sbuf("lams", [64, 2], F32)
        neglam = fw.sbuf("neglam", [64, 1], F32)
        gsub = fw.sbuf("gsub", [64, 1], F32)

        qA = [fw.sbuf(f"qA{i}", [128, 512], BF16) for i in range(4)]
        qB = [fw.sbuf(f"qB{i}", [128, 512], BF16) for i in range(4)]
        qm = [fw.sbuf(f"qm{i}", [128, 512], BF16) for i in range(2)]
        pT = [fw.sbuf(f"pT{i}", [128, 512], BF16) for i in range(3)]
        rd = fw.sbuf("rd", [128, 512], F32)
        bcs = fw.sbuf("bcs", [64, 512], F32)
        oj = [fw.sbuf(f"oj{i}", [64, 512], F32) for i in range(2)]
        yh = [fw.sbuf(f"yh{i}", [64, 512], F32) for i in range(4)]
        sq = [fw.sbuf(f"sq{i}", [128, 512], BF16) for i in range(4)]
        rs = fw.sbuf("rs", [128, 512], F32)
        catT = fw.sbuf("catT", [128, 14, 512], BF16)
        f1 = [fw.sbuf(f"f1_{i}", [128, 544], F32) for i in range(6)]
        f2 = [fw.sbuf(f"f2_{i}", [128, 544], F32) for i in range(4)]
        hb = [fw.sbuf(f"hb{i}", [128, 544], BF16) for i in range(2)]
        yb = [fw.sbuf(f"yb{i}", [128, 512], F32) for i in range(2)]
        pb = [fw.sbuf(f"pb{i}", [128, 512], BF16) for i in range(2)]
        xr = [fw.sbuf(f"xr{i}", [128, D], F32) for i in range(2)]

        sps = [fw.psum(f"sps{i}", [128, 512], F32) for i in range(3)]
        ops = [fw.psum(f"ops{i}", [128, 512], F32) for i in range(2)]
        gps = [fw.psum(f"gps{i}", [128, 512], F32) for i in range(3)]

        wo_rows = [(0, 128), (128, 128), (256, 128), (384, 128), (512, 64), (576, 64), (640, 64), (704, 64),
                   (768, 128), (896, 128), (1024, 64), (1088, 64), (1152, 64), (1216, 64)]
        for ci, (r0, n) in enumerate(wo_rows):
            sg = stgo[ci % 2]
            fw.dma(sg[0:n, :], P["w_out"][l, r0:r0 + n, :])
            fw.copy(pool if ci % 2 == 0 else act, Wo[0:n, ci, :], sg[0:n, :])
            fw.dma(gn[0:n, ci:ci + 1], P["group_norm"][l, r0:r0 + n].rr("(d o) -> d o", o=1))
        fw.memset(dve, BDf[:], 0.0)
        for g in range(4):
            pr = (g % 2) * 64
            fw.dma(BDf[pr:pr + 64, g // 2, pr:pr + 64], P["pool_w"][l, g])
        fw.copy(dve, BD[:], BDf[:])
        load_T(pscale[:], P["pool_scale"][l].rr("(t p) -> t p", p=128), 2, gps[0])
        load_T(lng[:], P["conv_ln_g"][l].rr("(t p) -> t p", p=128), 2, gps[1])
        load_T(lnb[:], P["conv_ln_b"][l].rr("(t p) -> t p", p=128), 2, gps[2])
        for t in range(2):
            load_T(cdw[:, t, :], P["conv_dw"][l, :, t * 128:(t + 1) * 128], 31, gps[t])
            load_T(scw[:, t, :], P["sconv_w"][l, :, t * 128:(t + 1) * 128], 3, gps[2])
        for t in range(2):
            fw.dma(stgo[t][:, 0:256], P["conv_pw"][l, t * 128:(t + 1) * 128, :])
            fw.copy(dve, pw[:, t, :], stgo[t][:, 0:256])
            for k in range(31):
                fw.ts(dve if k % 2 == 0 else pool, diag[:, t, k, :], ident[:], cdw[:, t, k:k + 1], None, ALU.mult)
        fw.dma(lamt[:], P["diff_lambda"][l].m(lambda a: a.unsqueeze(0).broadcast_to([64, 4, 32])))
        fw.tt(dve, lamp[:], lamt[:, 0:4:2, :], lamt[:, 1:4:2, :], ALU.mult)
        fw.reduce(lams[:], lamp[:], ALU.add)
        fw.actv(lams[:], lams[:], AF.Exp)
        fw.tt(dve, neglam[:], lams[:, 1:2], lams[:, 0:1], ALU.subtract)
        fw.ts(dve, neglam[:], neglam[:], -lam_init, None, ALU.add)
        fw.dma(gsub[:], P["diff_subln"][l].rr("(d o) -> d o", o=1))
        fw.ts(dve, gsub[:], gsub[:], 1.0 - lam_init, None, ALU.mult)

        def group_norm(ys, cidx, Pn):
            n = len(ys)
            gp = gps[0]
            for i, y in enumerate(ys):
                fw.actv(sq[i][0:Pn, :], y, AF.Square)
            for i in range(n):
                fw.mm(gp[0:Pn, :], ones[0:Pn, 0:Pn], sq[i][0:Pn, :], start=(i == 0), stop=(i == n - 1))
            fw.rsqrt(rs[0:Pn, :], gp[0:Pn, :], 1.0 / GW, EPS)
            for i, y in enumerate(ys):
                fw.stt(catT[0:Pn, cidx[i], :], y, gn[0:Pn, cidx[i]:cidx[i] + 1], rs[0:Pn, :], ALU.mult, ALU.mult)

        def attn_head(qa, qb, kt_tile, r0, nd, nkt, Vt, h, qc, dm, out_sb):
            op_ = ops[attn_head.n % 2]
            attn_head.n += 1
            na = nd + (4 if dm is not None else 0)
            ra = slice(r0, r0 + na)
            rdd = slice(r0, r0 + nd)
            for kt in range(nkt):
                sp_ = sps[kt % 3]
                kc_ = slice(kt * 128, (kt + 1) * 128)
                rel = kt - 4 * qc
                if dm is None or rel < 0:
                    fw.mm(sp_[:], kt_tile[ra, kc_], qa[ra, :])
                elif rel > 3:
                    fw.mm(sp_[:], kt_tile[ra, kc_], qb[ra, :])
                else:
                    if rel > 0:
                        fw.mm(sp_[:, 0:rel * 128], kt_tile[ra, kc_], qb[ra, 0:rel * 128], signal=False)
                    dc = slice(rel * 128, (rel + 1) * 128)
                    fw.mm(sp_[:, dc], kt_tile[rdd, kc_], qa[rdd, dc], start=True, stop=False, signal=False)
                    fw.mm(sp_[:, dc], ident[:], dm, start=False, stop=True, signal=(rel == 3))
                    if rel < 3:
                        fw.mm(sp_[:, (rel + 1) * 128:512], kt_tile[ra, kc_], qa[ra, (rel + 1) * 128:512])
                p_ = pT[kt % 3]
                fw.actv(p_[:], sp_[:], AF.Exp)
                fw.mm(op_[0:65, :], Vt[:, kt, h, 0:65], p_[:], start=(kt == 0), stop=(kt == nkt - 1))
            fw.recip(rd[64:65, :], op_[64:65, :])
            bp = gps[1]
            fw.mm(bp[0:64, :], onesf[64:65, 0:64], rd[64:65, :])
            fw.copy(act, bcs[:], bp[0:64, :])
            fw.tt(dve, out_sb, op_[0:64, :], bcs[:], ALU.mult)

        attn_head.n = 0

        for c in range(NCH):
            c0 = c * 512
            for hp in range(4):
                fw.dma(qA[hp][:], qTd[hp, :, c0:c0 + 512])
                fw.dma(qB[hp][:], qTd[hp, :, c0:c0 + 512])
                for side in range(2):
                    h = 2 * (hp % 2) + side
                    r = slice(side * 64 + 32, side * 64 + 36)
                    fw.dma(qA[hp][r, :], C["c_aq"][h, :, c0:c0 + 512])
                    fw.dma(qB[hp][r, :], C["c_aqn"][h, :, c0:c0 + 512])
            for hp in range(2):
                fw.dma(qm[hp][:], qmTd[hp, :, c0:c0 + 512])

            for h in range(4):
                for j in range(2):
                    hp = j * 2 + h // 2
                    side = h % 2
                    attn_head(qA[hp], qB[hp], kT[hp], side * 64, 32, NKT, Vaug, h, c, dmat[:, h, :], oj[j][:])
                fw.stt(yh[h][:], oj[1][:], neglam[:], oj[0][:], ALU.mult, ALU.add)
                fw.actv(sq[0][0:64, :], yh[h][:], AF.Square)
                fw.mm(gps[2][0:64, :], ones[0:64, 0:64], sq[0][0:64, :])
                fw.rsqrt(rs[0:64, :], gps[2][0:64, :], 1.0 / 64, EPS)
                fw.stt(yh[h][:], yh[h][:], gsub[:], rs[0:64, :], ALU.mult, ALU.mult)
            group_norm([yh[h][:] for h in range(4)], [4, 5, 6, 7], 64)

            for h in range(4):
                hp, side = h // 2, h % 2
                attn_head(qm[hp], qm[hp], kmT[hp], side * 64, 64, 2, Vm, h, c, None, yh[h][:])
            group_norm([yh[h][:] for h in range(4)], [10, 11, 12, 13], 64)

            lo = max(c0 - 8, 0)
            hi_ = min(c0 + 520, S)
            ys = []
            for t in range(2):
                u = f1[t]
                if c == 0 or c == NCH - 1:
                    fw.memset(dve, u[:, 0:528], 0.0)
                fw.dma(u[:, lo - (c0 - 8):hi_ - (c0 - 8)], projT[t, :, lo:hi_])
                s2, s4, s8, s16 = f1[2], f1[3], f1[4], f1[5]
                fw.tt(dve, s2[:, 1:528], u[:, 0:527], u[:, 1:528], ALU.add)
                fw.tt(dve, s4[:, 2:527], s2[:, 1:526], s2[:, 3:528], ALU.add)
                fw.tt(dve, s8[:, 4:525], s4[:, 2:523], s4[:, 6:527], ALU.add)
                fw.tt(dve, s16[:, 8:520], s8[:, 4:516], s8[:, 12:524], ALU.add)
                srcs = (s2, s4) if t == 0 else (s8, s16)
                for g2 in range(2):
                    pr = slice(g2 * 64, g2 * 64 + 64)
                    sw = srcs[g2]
                    if c == 0:
                        fw.tt(dve, sw[pr, 8:16], sw[pr, 8:16], edge[pr, t, 0:8], ALU.mult)
                    if c == NCH - 1:
                        fw.tt(dve, sw[pr, 512:520], sw[pr, 512:520], edge[pr, t, 8:16], ALU.mult)
                    fw.stt(pb[t][pr, :], sw[pr, 8:520], invw[pr, t, :], u[pr, 8:520], ALU.mult, ALU.subtract)
                fw.mm(gps[2][:], BD[:, t, :], pb[t][:])
                fw.actv(yb[t][:], gps[2][:], AF.Copy, scale=pscale[:, t:t + 1])
                ys.append(yb[t][:])
            group_norm(ys, [0, 1], 128)

            lo = max(c0 - 1, 0)
            hi_ = min(c0 + 513, S)
            ys = []
            for t in range(2):
                hh, bb, cc, z, cv = f1[0], f1[1], f1[2], f1[3], f1[4]
                if c == 0 or c == NCH - 1:
                    fw.memset(dve, hh[:, 0:514], 0.0)
                    fw.memset(dve, cc[:, 0:514], 0.0)
                fw.dma(hh[:, lo - (c0 - 1):hi_ - (c0 - 1)], projT[6 + t, :, lo:hi_])
                fw.dma(bb[:, 0:512], projT[8 + t, :, c0:c0 + 512])
                fw.dma(cc[:, lo - (c0 - 1):hi_ - (c0 - 1)], projT[10 + t, :, lo:hi_])
                fw.tt(dve, z[:, 0:514], hh[:, 0:514], cc[:, 0:514], ALU.mult)
                fw.ts(dve, cv[:, 0:512], z[:, 0:512], scw[:, t, 0:1], None, ALU.mult)
                fw.stt(cv[:, 0:512], z[:, 1:513], scw[:, t, 1:2], cv[:, 0:512], ALU.mult, ALU.add)
                fw.stt(cv[:, 0:512], z[:, 2:514], scw[:, t, 2:3], cv[:, 0:512], ALU.mult, ALU.add)
                fw.tt(dve, yb[t][:], bb[:, 0:512], cv[:, 0:512], ALU.mult)
                ys.append(yb[t][:])
            group_norm(ys, [8, 9], 128)

            lo = max(c0 - 15, 0)
            hi_ = min(c0 + 527, S)
            cvs = []
            for t in range(2):
                a_, g_ = f2[t * 2], f2[t * 2 + 1]
                if c == 0 or c == NCH - 1:
                    fw.memset(dve, a_[:, 0:542], 0.0)
                    fw.memset(dve, g_[:, 0:542], 0.0)
                fw.dma(a_[:, lo - (c0 - 15):hi_ - (c0 - 15)], projT[2 + t, :, lo:hi_])
                fw.dma(g_[:, lo - (c0 - 15):hi_ - (c0 - 15)], projT[4 + t, :, lo:hi_])
                fw.actv(g_[:, 0:542], g_[:, 0:542], AF.Sigmoid)
                fw.tt(dve, hb[t][:, 0:542], a_[:, 0:542], g_[:, 0:542], ALU.mult)
                cp = sps[t]
                for k in range(31):
                    fw.mm(cp[:], diag[:, t, k, :], hb[t][:, k:k + 512], start=(k == 0), stop=(k == 30))
                cvs.append(cp)
            for t in range(2):
                fw.copy(act, pb[t][:], cvs[t][:])
                fw.actv(sq[t][:], cvs[t][:], AF.Square)
            for t in range(2):
                fw.mm(gps[0][:], ones[:], pb[t][:], start=(t == 0), stop=(t == 1))
            for t in range(2):
                fw.mm(gps[1][:], ones[:], sq[t][:], start=(t == 0), stop=(t == 1))
            mu, var = f1[0], f1[1]
            fw.ts(dve, mu[:, 0:512], gps[0][:], 1.0 / GW, None, ALU.mult)
            fw.tt(dve, var[:, 0:512], mu[:, 0:512], mu[:, 0:512], ALU.mult)
            fw.stt(var[:, 0:512], gps[1][:], 1.0 / GW, var[:, 0:512], ALU.mult, ALU.subtract)
            fw.ts(dve, var[:, 0:512], var[:, 0:512], 0.0, None, ALU.max)
            fw.rsqrt(var[:, 0:512], var[:, 0:512], 1.0, EPS)
            for t in range(2):
                xc = f1[2 + t]
                fw.tt(dve, xc[:, 0:512], cvs[t][:], mu[:, 0:512], ALU.subtract)
                fw.tt(dve, xc[:, 0:512], xc[:, 0:512], var[:, 0:512], ALU.mult)
                fw.actv(hb[t][:, 0:512], xc[:, 0:512], AF.Silu, bias=lnb[:, t:t + 1], scale=lng[:, t:t + 1])
            ys = []
            for dt_ in range(2):
                yp = gps[2]
                for t in range(2):
                    fw.mm(yp[:], pw[:, t, dt_ * 128:(dt_ + 1) * 128], hb[t][:, 0:512], start=(t == 0), stop=(t == 1))
                fw.copy(act, yb[dt_][:], yp[:])
                ys.append(yb[dt_][:])
            group_norm(ys, [2, 3], 128)

            for sub in range(4):
                xx = xr[sub % 2]
                r0 = c0 + sub * 128
                fw.dma(xx[:], cur_x[r0:r0 + 128, :])
                for dh in range(2):
                    op_ = ops[dh]
                    for ci, (_, n) in enumerate(wo_rows):
                        fw.mm(op_[:], catT[0:n, ci, sub * 128:(sub + 1) * 128], Wo[0:n, ci, dh * 512:(dh + 1) * 512],
                              start=(ci == 0), stop=(ci == 13))
                    fw.tt(dve, xx[:, dh * 512:(dh + 1) * 512], xx[:, dh * 512:(dh + 1) * 512], op_[:], ALU.add)
                dstm = xmid if with_peer else dst_x
                fw.dma(dstm.k(("x", c, sub))[r0:r0 + 128, :], xx[:], q=stq)
        fw.pop()
        fw.pop()

        if with_peer:
            peer_phase(fw, nc, S, l, P, xmid, dst_x, ident, identf, iota, uT_d, v_d)
        cur_x = dst_x

    fw.barrier()
    fw.close()
    return nc, cs


def peer_phase(fw, nc, S, l, P, xmid, dst_x, ident, identf, iota, uT_d, v_d):
    import os
    STAGE = int(os.environ.get("PEER_STAGE", "9"))
    dve, act, pool, pe = fw.dve, fw.act, fw.pool, fw.pe
    stq = fw.act
    TCH = 256
    NTC = S // TCH
    NEG = -1e30

    fw.push()
    uf = [fw.sbuf(f"uf{i}", [128, D], F32) for i in range(2)]
    ub = [fw.sbuf(f"ub{i}", [128, 8, 128], BF16) for i in range(2)]
    vf = [fw.sbuf(f"vf{i}", [128, D], F32) for i in range(2)]
    vb = [fw.sbuf(f"vb{i}", [128, D], BF16) for i in range(2)]
    pt = [fw.psum(f"pt{i}", [128, 8, 128], F32) for i in range(2)]
    for c in range(128):
        i = c % 2
        fw.dma(uf[i][:], P["peer_u"][l, c * 128:(c + 1) * 128, :])
        for kc in range(8):
            fw.tr(pt[i][:, kc, :], uf[i][:, kc * 128:(kc + 1) * 128], identf[:], signal=(kc == 7))
        fw.copy(act, ub[i][:], pt[i][:])
        fw.dma(uT_d.k(("u", c))[c], ub[i][:], q=stq)
        fw.dma(vf[i][:], P["peer_v"][l, c * 128:(c + 1) * 128, :])
        fw.copy(pool, vb[i][:], vf[i][:])
        fw.dma(v_d.k(("v", c))[c * 128:(c + 1) * 128, :], vb[i][:], q=stq)
    fw.pop()
    if STAGE <= 1:
        fw.dma(dst_x[:], xmid[:])
        return

    fw.push()
    Wq = fw.sbuf("Wq", [128, 8, 2048], BF16)
    skT = fw.sbuf("skT", [128, 16, 128], BF16)
    gffn = fw.sbuf("gffn", [128, 8], F32)
    ldst = fw.sbuf("ldst", [8, 128], F32)
    bufA = fw.sbuf("bufA", [128, 2048], F32)
    bufB = fw.sbuf("bufB", [128, 2048], F32)
    bufC = fw.sbuf("bufC", [128, 2048], F32)
    xt2 = [fw.sbuf(f"xt2_{i}", [128, D], F32) for i in range(2)]
    xs2 = fw.sbuf("xs2", [128, D], BF16)
    ssv = fw.sbuf("ssv2", [128, 1], F32)
    xn2T = fw.sbuf("xn2T", [128, 8, TCH], BF16)
    qT = fw.sbuf("qT", [128, 16, TCH], BF16)
    a16 = fw.sbuf("a16", [128, 256], F32)
    ia = fw.sbuf("ia", [128, 256], U32)
    iaf = fw.sbuf("iaf", [128, 256], F32)
    best = fw.sbuf("best", [128, 128], F32)
    ci = fw.sbuf("ci", [128, 128], U32)
    eb = fw.sbuf("eb", [128, 128], F32)
    zs = fw.sbuf("zs", [128, 8], F32)
    gg = fw.sbuf("gg", [128, 128], F32)
    r1i = fw.sbuf("r1i", [128, 128], U32)
    r2i = fw.sbuf("r2i", [128, 128], U32)
    r1f = fw.sbuf("r1f", [128, 128], F32)
    r2f = fw.sbuf("r2f", [128, 128], F32)
    i1s = fw.sbuf("i1s", [128, 128], F32)
    i2s = fw.sbuf("i2s", [128, 128], F32)
    tT = fw.sbuf("tT", [128, 3, 128], F32)
    OHA = fw.sbuf("OHA", [128, 32, 128], BF16)
    OHB = fw.sbuf("OHB", [128, 32, 128], BF16)
    GT = fw.sbuf("GT", [128, 128, TCH], BF16)
    utb = [fw.sbuf(f"utb{i}", [128, 2, 8, 128], BF16) for i in range(2)]
    vtb = [fw.sbuf(f"vtb{i}", [128, 2, D], BF16) for i in range(2)]
    geb = [fw.sbuf(f"geb{i}", [128, TCH], BF16) for i in range(2)]
    wt = [fw.sbuf(f"wt{i}", [128, TCH], BF16) for i in range(2)]
    oacc = [fw.psum(f"oacc{i}", [128, D], F32) for i in range(2)]
    pbig = fw.psum("pbig", [128, 2048], F32)

    def bank(i):
        return pbig.k(("b", i))[:, i * 512:(i + 1) * 512]

    fw.dma(ldst[:], P["norm_ffn"][l].rr("(kc kp) -> kc kp", kp=128))
    fw.tr(bank(0)[:, 0:8], ldst[:], identf[0:8, 0:8])
    fw.copy(dve, gffn[:], bank(0)[:, 0:8])
    for kc in range(8):
        sg = bufA if kc % 2 == 0 else bufB
        fw.dma(sg[:], P["peer_wq"][l, kc * 128:(kc + 1) * 128, :])
        fw.copy(pool if kc % 2 == 0 else act, Wq[:, kc, :], sg[:])
    for hj in range(16):
        sg = bufC
        fw.dma(sg.k(hj % 4)[:, (hj % 4) * 128:(hj % 4 + 1) * 128], P["peer_subkeys"][l, hj // 2, hj % 2])
        b = bank(hj % 4)
        fw.tr(b[:, 0:128], sg.k(hj % 4)[:, (hj % 4) * 128:(hj % 4 + 1) * 128], identf[:])
        fw.copy(act if hj % 2 == 0 else dve, skT[:, hj, :], b[:, 0:128])

    if STAGE <= 2:
        fw.dma(dst_x[:], xmid[:])
        fw.pop()
        return
    a16v = a16[:].rr("p (h j r) -> p h j r", h=8, j=2)
    iafv = iaf[:].rr("p (h j r) -> p h j r", h=8, j=2)

    for tc in range(NTC):
        t0 = tc * TCH
        for sub in range(2):
            r0 = t0 + sub * 128
            fw.dma(xt2[sub][:], xmid[r0:r0 + 128, :])
            fw.actv(bufA[:, 0:D], xt2[sub][:], AF.Square, accum=ssv[:])
            fw.rsqrt(ssv[:], ssv[:], 1.0 / D, EPS)
            fw.actv(xs2[:], xt2[sub][:], AF.Copy, scale=ssv[:])
            tpb = bank(sub).m(lambda a: a.bitcast(BF16).rearrange("p (k t) -> p k t", k=8))
            for kc in range(8):
                fw.tr(tpb[:, kc, :], xs2[:, kc * 128:(kc + 1) * 128], ident[:], signal=(kc == 7))
            fw.tt(dve, xn2T[:, :, sub * 128:(sub + 1) * 128], tpb,
                  gffn[:].m(lambda a: a.unsqueeze(2).broadcast_to([128, 8, 128])), ALU.mult)
        for hj in range(16):
            b = bank(hj % 4)
            for kc in range(8):
                fw.mm(b[:, 0:TCH], Wq[:, kc, hj * 128:(hj + 1) * 128], xn2T[:, kc, :], start=(kc == 0), stop=(kc == 7))
            fw.copy(act if hj % 2 == 0 else dve, qT[:, hj, :], b[:, 0:TCH])
        for sub in range(2 if STAGE >= 4 else 0):
            tsl = slice(sub * 128, (sub + 1) * 128)
            for hj in range(16):
                fw.mm(pbig.k(("b", hj // 4))[:, hj * 128:(hj + 1) * 128], qT[:, hj, tsl], skT[:, hj, :])
            for i in range(4):
                fw.copy(act if i % 2 == 0 else dve, bufA.k(i)[:, i * 512:(i + 1) * 512], bank(i))
            for hj in range(16):
                sl = slice(hj * 128, (hj + 1) * 128)
                kA = bufA.k(hj // 4)
                kB = bufB.k(hj)
                a0 = a16.k(hj)[:, hj * 16:hj * 16 + 8]
                a1 = a16.k(hj)[:, hj * 16 + 8:hj * 16 + 16]
                fw.op(dve, lambda: nc.vector.max(a0.ap, kA[:, sl].ap), [kA[:, sl]], [a0])
                fw.op(dve, lambda: nc.vector.max_index(ia.k(hj)[:, hj * 16:hj * 16 + 8].ap, a0.ap, kA[:, sl].ap),
                      [kA[:, sl], a0], [ia.k(hj)[:, hj * 16:hj * 16 + 8]])
                fw.op(dve, lambda: nc.vector.match_replace(kB[:, sl].ap, a0.ap, kA[:, sl].ap, NEG), [kA[:, sl], a0], [kB[:, sl]])
                fw.op(dve, lambda: nc.vector.max(a1.ap, kB[:, sl].ap), [kB[:, sl]], [a1])
                fw.op(dve, lambda: nc.vector.max_index(ia.k(hj)[:, hj * 16 + 8:hj * 16 + 16].ap, a1.ap, kB[:, sl].ap),
                      [kB[:, sl], a1], [ia.k(hj)[:, hj * 16 + 8:hj * 16 + 16]])
            cand = bufB[:].rr("p (h a b) -> p h a b", h=8, a=16)
            fw.tt(dve, cand, a16v[:, :, 0, :].m(lambda a: a.unsqueeze(3).broadcast_to([128, 8, 16, 16])),
                  a16v[:, :, 1, :].m(lambda a: a.unsqueeze(2).broadcast_to([128, 8, 16, 16])), ALU.add)
            for h in range(8):
                sl = slice(h * 256, (h + 1) * 256)
                kB = bufB.k(("c", h))
                kC = bufC.k(("c", h))
                b0 = best.k(h)[:, h * 16:h * 16 + 8]
                b1 = best.k(h)[:, h * 16 + 8:h * 16 + 16]
                fw.op(dve, lambda: nc.vector.max(b0.ap, kB[:, sl].ap), [kB[:, sl]], [b0])
                fw.op(dve, lambda: nc.vector.max_index(ci.k(h)[:, h * 16:h * 16 + 8].ap, b0.ap, kB[:, sl].ap),
                      [kB[:, sl], b0], [ci.k(h)[:, h * 16:h * 16 + 8]])
                fw.op(dve, lambda: nc.vector.match_replace(kC[:, sl].ap, b0.ap, kB[:, sl].ap, NEG), [kB[:, sl], b0], [kC[:, sl]])
                fw.op(dve, lambda: nc.vector.max(b1.ap, kC[:, sl].ap), [kC[:, sl]], [b1])
                fw.op(dve, lambda: nc.vector.max_index(ci.k(h)[:, h * 16 + 8:h * 16 + 16].ap, b1.ap, kC[:, sl].ap),
                      [kC[:, sl], b1], [ci.k(h)[:, h * 16 + 8:h * 16 + 16]])
            bv = best[:].rr("p (h s) -> p h s", h=8)
            ev_ = eb[:].rr("p (h s) -> p h s", h=8)
            fw.tt(dve, ev_, bv, bv[:, :, 0:1].m(lambda a: a.broadcast_to([128, 8, 16])), ALU.subtract)
            fw.actv(eb[:], eb[:], AF.Exp)
            fw.reduce(zs[:], ev_, ALU.add)
            fw.recip(zs[:], zs[:])
            fw.tt(dve, gg[:].rr("p (h s) -> p h s", h=8), ev_, zs[:].m(lambda a: a.unsqueeze(2).broadcast_to([128, 8, 16])), ALU.mult)
            fw.copy(dve, iaf[:], ia[:])
            fw.ts(dve, r1i[:], ci[:], 4, None, ALU.logical_shift_right)
            fw.ts(dve, r2i[:], ci[:], 15, None, ALU.bitwise_and)
            fw.copy(dve, r1f[:], r1i[:])
            fw.copy(dve, r2f[:], r2i[:])
            for (rf, jj, dst) in ((r1f, 0, i1s), (r2f, 1, i2s)):
                oh = bufA[:].rr("p (a r) -> p a r", r=16)
                fw.tt(dve, oh, iota[:, 0:16].m(lambda a: a.unsqueeze(1).broadcast_to([128, 128, 16])),
                      rf[:].m(lambda a: a.unsqueeze(2).broadcast_to([128, 128, 16])), ALU.is_equal)
                oh4 = bufA[:].rr("p (h s r) -> p h s r", h=8, s=16)
                fw.tt(dve, oh4, oh4, iafv[:, :, jj, :].m(lambda a: a.unsqueeze(2).broadcast_to([128, 8, 16, 16])), ALU.mult)
                fw.reduce(dst[:], oh, ALU.add)
            b = bank(0)
            for i, src in enumerate((i1s, i2s, gg)):
                fw.tr(b[:, i * 128:(i + 1) * 128], src[:], identf[:], signal=(i == 2))
            fw.copy(act, tT[:], b[:, 0:384].rr("p (a t) -> p a t", a=3))
            for g4 in range(4):
                ts_ = slice(g4 * 32, (g4 + 1) * 32)
                io = iota[:].m(lambda a: a.unsqueeze(1).broadcast_to([128, 32, 128]))
                fw.tt(dve, OHA[:], io, tT[:, 0, ts_].m(lambda a: a.unsqueeze(2).broadcast_to([128, 32, 128])), ALU.is_equal)
                fw.tt(dve, OHA[:], OHA[:], tT[:, 2, ts_].m(lambda a: a.unsqueeze(2).broadcast_to([128, 32, 128])), ALU.mult)
                fw.tt(dve, OHB[:], io, tT[:, 1, ts_].m(lambda a: a.unsqueeze(2).broadcast_to([128, 32, 128])), ALU.is_equal)
                for q4 in range(8):
                    b = bank(1 + q4 % 2)
                    for k in range(4):
                        tk = q4 * 4 + k
                        fw.mm(b[:, k * 128:(k + 1) * 128], OHB[:, tk, :], OHA[:, tk, :], signal=(k == 3))
                    tg = sub * 128 + g4 * 32 + q4 * 4
                    fw.copy(act if q4 % 2 == 0 else dve, GT[:, :, tg:tg + 4].rr("p c t -> p t c"),
                            b[:].rr("p (t c) -> p t c", t=4))
        for cg in range(64 if STAGE >= 5 else 0):
            i = cg % 2
            fw.dma(utb[i][:], uT_d[cg * 2:(cg + 1) * 2].rr("c p k e -> p c k e"))
            fw.dma(vtb[i][:], v_d[cg * 256:(cg + 1) * 256, :].rr("(c p) d -> p c d", p=128))
            for cc in range(2):
                c = cg * 2 + cc
                b = bank(c % 2 + 2)
                for kc in range(8):
                    fw.mm(b[:, 0:TCH], utb[i][:, cc, kc, :], xn2T[:, kc, :], start=(kc == 0), stop=(kc == 7))
                fw.actv(geb[c % 2][:], b[:, 0:TCH], AF.Gelu if os.environ.get("DENSE_VAR", "0") != "1" else AF.Copy)
                fw.tt(dve, wt[c % 2][:], geb[c % 2][:], GT[:, c, :], ALU.mult)
                for sub in range(2 if os.environ.get("DENSE_VAR", "0") != "2" else 0):
                    for dh in range(2):
                        fw.mm(oacc[sub].k(dh)[:, dh * 512:(dh + 1) * 512], wt[c % 2][:, sub * 128:(sub + 1) * 128],
                              vtb[i][:, cc, dh * 512:(dh + 1) * 512], start=(c == 0), stop=(c == 127),
                              signal=(sub == 1 and dh == 1))
        for sub in range(2):
            r0 = t0 + sub * 128
            for dh in range(2 if (STAGE >= 5 and os.environ.get("DENSE_VAR", "0") not in ("2", "3")) else 0):
                fw.tt(dve, xt2[sub][:, dh * 512:(dh + 1) * 512], xt2[sub][:, dh * 512:(dh + 1) * 512],
                      oacc[sub].k(dh)[:, dh * 512:(dh + 1) * 512], ALU.add)
            fw.dma(dst_x.k(("o", tc, sub))[r0:r0 + 128, :], xt2[sub][:], q=stq)
    fw.pop()


_CACHE = {}


def kernel(**inputs):
    n = 8
    if "nc" not in _CACHE:
        _CACHE["nc"] = build()
    nc, cs = _CACHE["nc"]
    in_maps = []
    for b in range(n):
        m = {"x": np.ascontiguousarray(inputs["x"][b]), "mem": np.ascontiguousarray(inputs["mem"][b])}
        for k in PARAM_SHAPES:
            m[k] = np.ascontiguousarray(inputs[k])
        m.update(cs)
        in_maps.append(m)
    res = run_bass_kernel_spmd(nc, in_maps, core_ids=list(range(n)))
    return np.stack([r["out"] for r in res.results], axis=0).astype(np.float32)
```

```python
import math
import numpy as np
import ml_dtypes
import concourse.bass as bass
import concourse.mybir as mybir
from concourse.bass_utils import run_bass_kernel_spmd

F32 = mybir.dt.float32
BF16 = mybir.dt.bfloat16
I32 = mybir.dt.int32
U32 = mybir.dt.uint32
AF = mybir.ActivationFunctionType
ALU = mybir.AluOpType
AX = mybir.AxisListType

D = 1024
GW = 256
EPS = 1e-6
NMEM = 256
SEQ = 4096
DEPTH = 4
NKEYS = 128
NEXP = 16384
import os as _os
NO_SELF_SYNC = tuple(x for x in _os.environ.get("NO_SELF_SYNC", "").split(",") if x)


class V:
    def __init__(self, ap, t, key=None):
        self.ap = ap
        self.t = t
        self.key = key

    def __getitem__(self, idx):
        return V(self.ap[idx], self.t, self.key)

    def rr(self, pat, **kw):
        return V(self.ap.rearrange(pat, **kw), self.t, self.key)

    def bc(self, shape):
        return V(self.ap.broadcast_to(list(shape)), self.t, self.key)

    def m(self, f):
        return V(f(self.ap), self.t, self.key)


class T:
    def __init__(self, root, name, is_ap=False):
        self.root = root
        self.name = name
        self.st = {}

    def __getitem__(self, idx):
        return V(self.root[idx], self, None)

    def k(self, key):
        t = self

        class _K:
            def __getitem__(self, idx):
                return V(t.root[idx], t, key)

        return _K()


class Eng:
    def __init__(self, fw, name, hw):
        self.fw = fw
        self.name = name
        self.hw = hw
        self.waited = {}
        self.sem = fw.new_sem(name)
        self.count = 0

    def wait(self, ev, force=False):
        if ev is None:
            return
        sem, val = ev
        k = id(sem)
        if sem is self.sem and (val > self.count or self.name == "pe"):
            return
        if sem is self.sem and not force and self.name in NO_SELF_SYNC:
            return
        if self.waited.get(k, 0) >= val:
            return
        self.hw.wait_ge(sem, val)
        self.waited[k] = val
        self.fw.n_waits += 1


class FW:
    def __init__(self, nc, n_dma_sems=32):
        self.nc = nc
        self._stack = []
        self._scopes = []
        self.n_waits = 0
        self.n_inst = 0
        self.tiles = []
        self.pe = Eng(self, "pe", nc.tensor)
        self.act = Eng(self, "act", nc.scalar)
        self.dve = Eng(self, "dve", nc.vector)
        self.pool = Eng(self, "pool", nc.gpsimd)
        self.sp = Eng(self, "sp", nc.sync)
        self.engs = [self.pe, self.act, self.dve, self.pool, self.sp]
        self.dma_pools = {
            "sp": [[[self.new_sem(f"dsp{i}"), 0] for i in range(24)], 0],
            "act": [[[self.new_sem(f"dac{i}"), 0] for i in range(8)], 0],
        }

    def new_sem(self, name):
        cm = self.nc.semaphore(name)
        s = cm.__enter__()
        self._stack.append(cm)
        return s

    def _alloc(self, cm, name):
        t = cm.__enter__()
        if self._scopes:
            self._scopes[-1].append(cm)
        else:
            self._stack.append(cm)
        tt = T(t, name)
        self.tiles.append(tt)
        return tt

    def _uname(self, name):
        self.uid = getattr(self, "uid", 0) + 1
        return f"{name}_{self.uid}"

    def sbuf(self, name, shape, dtype):
        name = self._uname(name)
        return self._alloc(self.nc.sbuf_tensor(name, list(shape), dtype), name)

    def psum(self, name, shape, dtype=F32):
        name = self._uname(name)
        return self._alloc(self.nc.psum_tensor(name, list(shape), dtype), name)

    def dram(self, name, shape, dtype, kind="Internal"):
        t = self.nc.dram_tensor(name, list(shape), dtype, kind=kind)
        tt = T(t.ap(), name)
        self.tiles.append(tt)
        return tt

    def push(self):
        self._scopes.append([])

    def pop(self):
        self.barrier()
        for cm in reversed(self._scopes.pop()):
            cm.__exit__(None, None, None)

    def close(self):
        while self._stack:
            self._stack.pop().__exit__(None, None, None)

    def barrier(self):
        evs = [(e.sem, e.count) for e in self.engs if e.count > 0]
        for pl, _ in self.dma_pools.values():
            evs += [(s, v) for s, v in pl if v > 0]
        for e in self.engs:
            for ev in evs:
                e.wait(ev)
        for t in self.tiles:
            t.st = {}

    def _deps(self, eng, reads, writes, force=False):
        for v in reads:
            for k, st in v.t.st.items():
                if v.key is None or k is None or k == v.key:
                    eng.wait(st[0], force)
        for v in writes:
            for k, st in v.t.st.items():
                if v.key is None or k is None or k == v.key:
                    eng.wait(st[0], force)
                    for ev in st[1].values():
                        eng.wait(ev, force)

    def _mark(self, ev, reads, writes):
        sem, val = ev
        for v in reads:
            st = v.t.st.setdefault(v.key, [None, {}])
            st[1][id(sem)] = ev
        for v in writes:
            if v.key is None:
                v.t.st = {None: [ev, {}]}
            else:
                v.t.st[v.key] = [ev, {}]

    def op(self, eng, fn, reads=(), writes=(), signal=True):
        reads = [r for r in reads if isinstance(r, V)]
        self._deps(eng, reads, writes)
        ins = fn()
        self.n_inst += 1
        if signal:
            ins.then_inc(eng.sem, 1)
            eng.count += 1
            ev = (eng.sem, eng.count)
        else:
            ev = (eng.sem, eng.count + 1)
        self._mark(ev, reads, writes)
        return ev

    def dma(self, out, in_, q=None, **kw):
        q = q or self.sp
        self._deps(q, [in_], [out], force=True)
        pl = self.dma_pools[q.name]
        slot = pl[0][pl[1]]
        pl[1] = (pl[1] + 1) % len(pl[0])
        sem, val = slot
        if val > 0:
            q.wait((sem, val))
        ins = q.hw.dma_start(out=out.ap, in_=in_.ap, **kw)
        ins.then_inc(sem, 16)
        slot[1] = val + 16
        ev = (sem, val + 16)
        self.n_inst += 1
        self._mark(ev, [in_], [out])
        return ev

    @staticmethod
    def _a(x):
        return x.ap if isinstance(x, V) else x

    def mm(self, out, lhsT, rhs, start=True, stop=True, signal=None):
        if signal is None:
            signal = stop
        return self.op(self.pe, lambda: self.nc.tensor.matmul(out.ap, lhsT=lhsT.ap, rhs=rhs.ap, start=start, stop=stop),
                       [lhsT, rhs], [out], signal=signal)

    def tr(self, out, in_, ident, signal=True):
        return self.op(self.pe, lambda: self.nc.tensor.transpose(out.ap, in_.ap, ident.ap), [in_, ident], [out], signal=signal)

    def actv(self, out, in_, func, bias=None, scale=None, accum=None):
        kw = {}
        if bias is not None:
            kw["bias"] = self._a(bias)
        if scale is not None:
            kw["scale"] = self._a(scale)
        w = [out]
        if accum is not None:
            kw["accum_out"] = accum.ap
            w.append(accum)
        return self.op(self.act, lambda: self.nc.scalar.activation(out.ap, in_.ap, func, **kw), [in_, bias, scale], w)

    def _ve(self, eng):
        return self.nc.vector if eng is self.dve else self.nc.gpsimd

    def ts(self, eng, out, in0, s1, s2, op0, op1=None, accum=None):
        kw = {}
        if op1 is not None:
            kw["op1"] = op1
        w = [out]
        if accum is not None:
            kw["accum_out"] = accum.ap
            w.append(accum)
        return self.op(eng, lambda: self._ve(eng).tensor_scalar(out.ap, in0.ap, self._a(s1), self._a(s2), op0, **kw),
                       [in0, s1, s2], w)

    def tt(self, eng, out, in0, in1, op):
        return self.op(eng, lambda: self._ve(eng).tensor_tensor(out.ap, in0.ap, in1.ap, op), [in0, in1], [out])

    def stt(self, out, in0, scalar, in1, op0, op1):
        return self.op(self.dve, lambda: self.nc.vector.scalar_tensor_tensor(out.ap, in0.ap, self._a(scalar), in1.ap, op0, op1),
                       [in0, scalar, in1], [out])

    def copy(self, eng, out, in_):
        if eng is self.act:
            return self.op(eng, lambda: self.nc.scalar.copy(out.ap, in_.ap), [in_], [out])
        return self.op(eng, lambda: self._ve(eng).tensor_copy(out.ap, in_.ap), [in_], [out])

    def recip(self, out, in_):
        return self.op(self.dve, lambda: self.nc.vector.reciprocal(out.ap, in_.ap), [in_], [out])

    def reduce(self, out, in_, op, axis=AX.X):
        return self.op(self.dve, lambda: self.nc.vector.tensor_reduce(out.ap, in_.ap, axis, op), [in_], [out])

    def memset(self, eng, out, val):
        return self.op(eng, lambda: self._ve(eng).memset(out.ap, val), [], [out])

    def rsqrt(self, out, in_, scale, eps):
        self.actv(out, in_, AF.Sqrt, bias=eps, scale=scale)
        self.recip(out, out)


def make_consts(S):
    bf = ml_dtypes.bfloat16
    pos = np.arange(S)
    hi = (pos // 64).astype(np.float64)
    lo = (pos % 64).astype(np.float64)
    slopes = [2.0 ** (-8.0 * (h + 1) / 4) for h in range(4)]
    aq = np.zeros((4, 4, S), np.float64)
    ak = np.zeros((4, 4, S), np.float64)
    dmat = np.zeros((4, 128, 128), np.float64)
    dd = np.abs(np.arange(128)[:, None] - np.arange(128)[None, :])
    for h, sl in enumerate(slopes):
        aq[h, 0] = 64 * sl * hi
        aq[h, 1] = sl * lo
        aq[h, 2] = 1
        aq[h, 3] = 1
        ak[h, 0] = -1
        ak[h, 1] = -1
        ak[h, 2] = 64 * sl * hi
        ak[h, 3] = sl * lo
        dmat[h] = -sl * dd
    edge = np.ones((2, 128, 16), np.float64)
    for g, w in enumerate((2, 4, 8, 16)):
        tl, pr = g // 2, (g % 2) * 64
        for i in range(8):
            t = i
            cnt = min(t + w // 2, S) - max(t - w // 2, 0)
            edge[tl, pr:pr + 64, i] = w / cnt
            t = S - 8 + i
            cnt = min(t + w // 2, S) - max(t - w // 2, 0)
            edge[tl, pr:pr + 64, 8 + i] = w / cnt
    invw = np.zeros((2, 128, 1), np.float64)
    for g, w in enumerate((2, 4, 8, 16)):
        invw[g // 2, (g % 2) * 64:(g % 2) * 64 + 64, 0] = 1.0 / w
    p = np.arange(128)
    blk32 = (p[:, None] // 32 == p[None, :] // 32).astype(np.float64)
    blk64 = (p[:, None] // 64 == p[None, :] // 64).astype(np.float64)
    return {
        "c_aq": aq.astype(bf), "c_aqn": (-aq).astype(bf), "c_ak": ak.astype(bf),
        "c_dmat": dmat.transpose(1, 0, 2).copy().astype(bf),
        "c_edge": edge.transpose(1, 0, 2).copy().astype(np.float32),
        "c_invw": invw.transpose(1, 0, 2).copy().astype(np.float32),
        "c_blk32": blk32.astype(bf), "c_blk64": blk64.astype(bf),
        "c_ident": np.eye(128).astype(bf), "c_identf": np.eye(128).astype(np.float32),
        "c_iota": np.tile(np.arange(128, dtype=np.float32)[None, :], (128, 1)),
    }


PARAM_SHAPES = {
    "norm_mix": (D,), "w_in": (D, 2560), "pool_w": (4, 64, 64), "pool_scale": (GW,), "conv_dw": (31, GW),
    "conv_ln_g": (GW,), "conv_ln_b": (GW,), "conv_pw": (GW, GW), "diff_qk_norm": (2, 32), "diff_lambda": (4, 32),
    "diff_subln": (64,), "sconv_w": (3, GW), "mem_norm": (D,), "w_mem_kv": (D, 512), "mem_qk_norm": (2, 64),
    "group_norm": (1280,), "w_out": (1280, D), "norm_ffn": (D,), "peer_wq": (D, 2048),
    "peer_subkeys": (8, 2, 128, 128), "peer_u": (NEXP, D), "peer_v": (NEXP, D),
}


def build(S=SEQ, depth=DEPTH, with_peer=True):
    nc = bass.Bass("TRN2", target_bir_lowering=False)
    fw = FW(nc)
    NCH = S // 512
    NKT = S // 128
    cs = make_consts(S)

    def ext(name, shape, dt=F32):
        return T(nc.dram_tensor(name, list(shape), dt, kind="ExternalInput").ap(), name)

    x_in = ext("x", (S, D))
    mem_in = ext("mem", (NMEM, D))
    P = {k: ext(k, (depth,) + shp) for k, shp in PARAM_SHAPES.items()}
    C = {k: ext(k, v.shape, BF16 if v.dtype == ml_dtypes.bfloat16 else F32) for k, v in cs.items()}
    out_d = T(nc.dram_tensor("out", [S, D], F32, kind="ExternalOutput").ap(), "out")

    xbuf = fw.dram("xbuf", (S, D), F32)
    xmid = fw.dram("xmid", (S, D), F32)
    projT = fw.dram("projT", (14, 128, S), F32)
    qTd = fw.dram("qTd", (4, 128, S), BF16)
    qmTd = fw.dram("qmTd", (2, 128, S), BF16)
    uT_d = fw.dram("uT_d", (128, 128, 8, 128), BF16)
    v_d = fw.dram("v_d", (NEXP, D), BF16)

    dve, act, pool, pe = fw.dve, fw.act, fw.pool, fw.pe
    stq = fw.act

    ident = fw.sbuf("ident", [128, 128], BF16)
    identf = fw.sbuf("identf", [128, 128], F32)
    ones = fw.sbuf("ones", [128, 128], BF16)
    blk32 = fw.sbuf("blk32", [128, 128], BF16)
    blk64 = fw.sbuf("blk64", [128, 128], BF16)
    dmat = fw.sbuf("dmat", [128, 4, 128], BF16)
    edge = fw.sbuf("edge", [128, 2, 16], F32)
    invw = fw.sbuf("invw", [128, 2, 1], F32)
    iota = fw.sbuf("iota", [128, 128], F32)
    fw.dma(ident[:], C["c_ident"][:])
    fw.dma(identf[:], C["c_identf"][:])
    fw.dma(blk32[:], C["c_blk32"][:])
    fw.dma(blk64[:], C["c_blk64"][:])
    fw.dma(dmat[:], C["c_dmat"][:])
    fw.dma(edge[:], C["c_edge"][:])
    fw.dma(invw[:], C["c_invw"][:])
    fw.dma(iota[:], C["c_iota"][:])
    fw.memset(dve, ones[:], 1.0)
    onesf = fw.sbuf("onesf", [128, 64], F32)
    fw.memset(dve, onesf[:], 1.0)
    ldstg = fw.sbuf("ldstg", [32, 128], F32)

    def load_T(dst, src, n, ps):
        fw.dma(ldstg[0:n, :], src)
        fw.tr(ps[:, 0:n], ldstg[0:n, :], identf[0:n, 0:n])
        fw.copy(dve, dst, ps[:, 0:n])
    cur_x = x_in

    for l in range(depth):
        last = l == depth - 1
        lam_init = 0.8 - 0.6 * math.exp(-0.3 * l)
        dst_x = out_d if last else xbuf

        fw.push()
        kT = [fw.sbuf(f"kT{i}", [128, S], BF16) for i in range(4)]
        Vaug = fw.sbuf("Vaug", [128, NKT, 4, 66], BF16)
        kmT = [fw.sbuf(f"kmT{i}", [128, NMEM], BF16) for i in range(2)]
        Vm = fw.sbuf("Vm", [128, 2, 4, 66], BF16)
        for i in range(4):
            fw.memset(pool, kT[i][:], 0.0)
        fw.memset(dve, Vaug[:], 1.0)
        fw.memset(dve, Vm[:], 1.0)
        for hp in range(4):
            for side in range(2):
                h = 2 * (hp % 2) + side
                fw.dma(kT[hp][side * 64 + 32:side * 64 + 36, :], C["c_ak"][h])

        fw.push()
        W = fw.sbuf("W", [128, 8, 3072], BF16)
        stg = [fw.sbuf(f"stg{i}", [128, 2560], F32) for i in range(2)]
        gmix = fw.sbuf("gmix", [128, 8], F32)
        gmem = fw.sbuf("gmem", [128, 8], F32)
        gq = fw.sbuf("gq", [128, 1], F32)
        gk = fw.sbuf("gk", [128, 1], F32)
        gmq = fw.sbuf("gmq", [128, 1], F32)
        gmk = fw.sbuf("gmk", [128, 1], F32)
        Wkv = fw.sbuf("Wkv", [128, 8, 512], BF16)
        memnT = fw.sbuf("memnT", [128, 8, NMEM], BF16)
        xt = [fw.sbuf(f"xt{i}", [128, D], F32) for i in range(2)]
        xs = [fw.sbuf(f"xs{i}", [128, D], BF16) for i in range(2)]
        junk = fw.sbuf("junk", [128, D], F32)
        ssv = [fw.sbuf(f"ssv{i}", [128, 1], F32) for i in range(2)]
        xnT = [fw.sbuf(f"xnT{i}", [128, 8, 512], BF16) for i in range(2)]
        ev_f = [fw.sbuf(f"evf{i}", [128, 512], F32) for i in range(3)]
        sqb = [fw.sbuf(f"sqb{i}", [128, 512], BF16) for i in range(2)]
        rst = [fw.sbuf(f"rst{i}", [128, 512], F32) for i in range(2)]
        qn = [fw.sbuf(f"qn{i}", [128, 512], BF16) for i in range(2)]
        tp = [fw.psum(f"tp{i}", [128, 8, 128], BF16) for i in range(2)]
        pj = [fw.psum(f"pj{i}", [128, 512], F32) for i in range(3)]
        pss = [fw.psum(f"pss{i}", [128, 512], F32) for i in range(2)]

        fw.memset(pool, W[:], 0.0)
        fw.memset(dve, gq[:], 0.0)
        fw.memset(dve, gk[:], 0.0)
        load_T(gmix[:], P["norm_mix"][l].rr("(kc kp) -> kc kp", kp=128), 8, pj[0])
        load_T(gmem[:], P["mem_norm"][l].rr("(kc kp) -> kc kp", kp=128), 8, pj[1])
        for side in range(2):
            fw.dma(gq[side * 64:side * 64 + 32, :], P["diff_qk_norm"][l, 0].rr("(d o) -> d o", o=1))
            fw.dma(gk[side * 64:side * 64 + 32, :], P["diff_qk_norm"][l, 1].rr("(d o) -> d o", o=1))
            fw.dma(gmq[side * 64:side * 64 + 64, :], P["mem_qk_norm"][l, 0].rr("(d o) -> d o", o=1))
            fw.dma(gmk[side * 64:side * 64 + 64, :], P["mem_qk_norm"][l, 1].rr("(d o) -> d o", o=1))
        fw.ts(dve, gq[:], gq[:], 32 ** -0.5, None, ALU.mult)
        fw.ts(dve, gmq[:], gmq[:], 64 ** -0.5, None, ALU.mult)

        for kc in range(8):
            sg = stg[kc % 2]
            fw.dma(sg[:], P["w_in"][l, kc * 128:(kc + 1) * 128, :])
            e = pool if kc % 2 == 0 else act
            fw.copy(e, W[:, kc, 0:768], sg[:, 0:768])
            for off_in, off_out in ((768, 768), (1024, 1280)):
                fw.copy(e, W[:, kc, off_out:off_out + 512].rr("p (a b c) -> p a b c", a=4, b=2)[:, :, :, 0:32],
                        sg[:, off_in:off_in + 256].rr("p (a b c) -> p a b c", a=4, b=2))
            fw.copy(e, W[:, kc, 1792:3072], sg[:, 1280:2560])
        for kc in range(8):
            sg = stg[kc % 2]
            fw.dma(sg[:, 0:512], P["w_mem_kv"][l, kc * 128:(kc + 1) * 128, :])
            fw.copy(pool if kc % 2 == 0 else act, Wkv[:, kc, :], sg[:, 0:512])

        def norm_T(src_v, i, gcol, dstT, col0):
            fw.dma(xt[i][:], src_v)
            fw.actv(junk[:], xt[i][:], AF.Square, accum=ssv[i][:])
            fw.rsqrt(ssv[i][:], ssv[i][:], 1.0 / D, EPS)
            fw.actv(xs[i][:], xt[i][:], AF.Copy, scale=ssv[i][:])
            for kc in range(8):
                fw.tr(tp[i][:, kc, :], xs[i][:, kc * 128:(kc + 1) * 128], ident[:], signal=(kc == 7))
            fw.tt(dve, dstT[:, :, col0:col0 + 128], tp[i][:], gcol[:].m(lambda a: a.unsqueeze(2).broadcast_to([128, 8, 128])), ALU.mult)

        def qk_norm(ps, blk, gcol, nh, i):
            fw.actv(sqb[i][:], ps[:], AF.Square)
            fw.mm(pss[i][:], blk[:], sqb[i][:])
            fw.rsqrt(rst[i][:], pss[i][:], 1.0 / nh, EPS)

        for mt in range(2):
            norm_T(mem_in[mt * 128:(mt + 1) * 128, :], mt, gmem, memnT, mt * 128)
        for hp in range(2):
            ps = pj[hp]
            for kc in range(8):
                fw.mm(ps[:, 0:NMEM], Wkv[:, kc, hp * 128:(hp + 1) * 128], memnT[:, kc, :], start=(kc == 0), stop=(kc == 7))
            fw.actv(sqb[hp][:, 0:NMEM], ps[:, 0:NMEM], AF.Square)
            fw.mm(pss[hp][:, 0:NMEM], blk64[:], sqb[hp][:, 0:NMEM])
            fw.rsqrt(rst[hp][:, 0:NMEM], pss[hp][:, 0:NMEM], 1.0 / 64, EPS)
            fw.stt(kmT[hp][:], ps[:, 0:NMEM], gmk[:], rst[hp][:, 0:NMEM], ALU.mult, ALU.mult)
        for mt in range(2):
            ps = pj[2]
            for kc in range(8):
                fw.mm(ps[:, 0:256], memnT[:, kc, mt * 128:(mt + 1) * 128], Wkv[:, kc, 256:512], start=(kc == 0), stop=(kc == 7))
            fw.copy(act, Vm[:, mt, :, 0:64], ps[:, 0:256].rr("p (h e) -> p h e", h=4))

        conv_chunks = [0, 128, 256, 384, 512, 640] + [2048 + 128 * i for i in range(6)]
        for c in range(NCH):
            xT = xnT[c % 2]
            for sub in range(4):
                r0 = c * 512 + sub * 128
                norm_T(cur_x[r0:r0 + 128, :], sub % 2, gmix, xT, sub * 128)
            cnt = 0
            for ci, wo in enumerate(conv_chunks):
                ps = pj[cnt % 3]
                evb = ev_f[cnt % 3]
                cnt += 1
                for kc in range(8):
                    fw.mm(ps[:], W[:, kc, wo:wo + 128], xT[:, kc, :], start=(kc == 0), stop=(kc == 7))
                fw.copy(act, evb[:], ps[:])
                fw.dma(projT.k(("p", ci, c))[ci, :, c * 512:(c + 1) * 512], evb[:], q=stq)
            for hp in range(4):
                ps = pj[cnt % 3]
                cnt += 1
                i = hp % 2
                for kc in range(8):
                    fw.mm(ps[:], W[:, kc, 768 + hp * 128:768 + (hp + 1) * 128], xT[:, kc, :], start=(kc == 0), stop=(kc == 7))
                qk_norm(ps, blk32, gq, 32, i)
                fw.stt(qn[i][:], ps[:], gq[:], rst[i][:], ALU.mult, ALU.mult)
                fw.dma(qTd.k(("q", hp, c))[hp, :, c * 512:(c + 1) * 512], qn[i][:], q=stq)
            for hp in range(4):
                ps = pj[cnt % 3]
                cnt += 1
                i = hp % 2
                for kc in range(8):
                    fw.mm(ps[:], W[:, kc, 1280 + hp * 128:1280 + (hp + 1) * 128], xT[:, kc, :], start=(kc == 0), stop=(kc == 7))
                qk_norm(ps, blk32, gk, 32, i)
                for side in range(2):
                    r = slice(side * 64, side * 64 + 32)
                    fw.stt(kT[hp].k(("k", c))[r, c * 512:(c + 1) * 512], ps[r, :], gk[r, :], rst[i][r, :], ALU.mult, ALU.mult)
            for hp in range(2):
                ps = pj[cnt % 3]
                cnt += 1
                i = hp % 2
                for kc in range(8):
                    fw.mm(ps[:], W[:, kc, 2816 + hp * 128:2816 + (hp + 1) * 128], xT[:, kc, :], start=(kc == 0), stop=(kc == 7))
                qk_norm(ps, blk64, gmq, 64, i)
                fw.stt(qn[i][:], ps[:], gmq[:], rst[i][:], ALU.mult, ALU.mult)
                fw.dma(qmTd.k(("q", hp, c))[hp, :, c * 512:(c + 1) * 512], qn[i][:], q=stq)
            for sub in range(4):
                ps = pj[cnt % 3]
                cnt += 1
                for kc in range(8):
                    fw.mm(ps[:, 0:256], xT[:, kc, sub * 128:(sub + 1) * 128], W[:, kc, 1792:2048], start=(kc == 0), stop=(kc == 7))
                fw.copy(act, Vaug.k(("v", c, sub))[:, c * 4 + sub, :, 0:64], ps[:, 0:256].rr("p (h e) -> p h e", h=4))
        fw.pop()

        fw.push()
        Wo = fw.sbuf("Wo", [128, 14, D], BF16)
        stgo = [fw.sbuf(f"stgo{i}", [128, D], F32) for i in range(2)]
        gn = fw.sbuf("gn", [128, 14], F32)
        BD = fw.sbuf("BD", [128, 2, 128], BF16)
        BDf = fw.sbuf("BDf", [128, 2, 128], F32)
        pscale = fw.sbuf("pscale", [128, 2], F32)
        cdw = fw.sbuf("cdw", [128, 2, 31], F32)
        diag = fw.sbuf("diag", [128, 2, 31, 128], BF16)
        lng = fw.sbuf("lng", [128, 2], F32)
        lnb = fw.sbuf("lnb", [128, 2], F32)
        pw = fw.sbuf("pw", [128, 2, 256], BF16)
        scw = fw.sbuf("scw", [128, 2, 3], F32)
        lamt = fw.sbuf("lamt", [64, 4, 32], F32)
        lamp = fw.sbuf("lamp", [64, 2, 32], F32)
        lams = fw.sbuf("lams", [64, 2], F32)
        neglam = fw.sbuf("neglam", [64, 1], F32)
        gsub = fw.sbuf("gsub", [64, 1], F32)

        qA = [fw.sbuf(f"qA{i}", [128, 512], BF16) for i in range(4)]
        qB = [fw.sbuf(f"qB{i}", [128, 512], BF16) for i in range(4)]
        qm = [fw.sbuf(f"qm{i}", [128, 512], BF16) for i in range(2)]
        pT = [fw.sbuf(f"pT{i}", [128, 512], BF16) for i in range(3)]
        rd = fw.sbuf("rd", [128, 512], F32)
        bcs = fw.sbuf("bcs", [64, 512], F32)
        oj = [fw.sbuf(f"oj{i}", [64, 512], F32) for i in range(2)]
        yh = [fw.sbuf(f"yh{i}", [64, 512], F32) for i in range(4)]
        sq = [fw.sbuf(f"sq{i}", [128, 512], BF16) for i in range(4)]
        rs = fw.sbuf("rs", [128, 512], F32)
        catT = fw.sbuf("catT", [128, 14, 512], BF16)
        f1 = [fw.sbuf(f"f1_{i}", [128, 544], F32) for i in range(6)]
        f2 = [fw.sbuf(f"f2_{i}", [128, 544], F32) for i in range(4)]
        hb = [fw.sbuf(f"hb{i}", [128, 544], BF16) for i in range(2)]
        yb = [fw.sbuf(f"yb{i}", [128, 512], F32) for i in range(2)]
        pb = [fw.sbuf(f"pb{i}", [128, 512], BF16) for i in range(2)]
        xr = [fw.sbuf(f"xr{i}", [128, D], F32) for i in range(2)]

        sps = [fw.psum(f"sps{i}", [128, 512], F32) for i in range(3)]
        ops = [fw.psum(f"ops{i}", [128, 512], F32) for i in range(2)]
        gps = [fw.psum(f"gps{i}", [128, 512], F32) for i in range(3)]

        wo_rows = [(0, 128), (128, 128), (256, 128), (384, 128), (512, 64), (576, 64), (640, 64), (704, 64),
                   (768, 128), (896, 128), (1024, 64), (1088, 64), (1152, 64), (1216, 64)]
        for ci, (r0, n) in enumerate(wo_rows):
            sg = stgo[ci % 2]
            fw.dma(sg[0:n, :], P["w_out"][l, r0:r0 + n, :])
            fw.copy(pool if ci % 2 == 0 else act, Wo[0:n, ci, :], sg[0:n, :])
            fw.dma(gn[0:n, ci:ci + 1], P["group_norm"][l, r0:r0 + n].rr("(d o) -> d o", o=1))
        fw.memset(dve, BDf[:], 0.0)
        for g in range(4):
            pr = (g % 2) * 64
            fw.dma(BDf[pr:pr + 64, g // 2, pr:pr + 64], P["pool_w"][l, g])
        fw.copy(dve, BD[:], BDf[:])
        load_T(pscale[:], P["pool_scale"][l].rr("(t p) -> t p", p=128), 2, gps[0])
        load_T(lng[:], P["conv_ln_g"][l].rr("(t p) -> t p", p=128), 2, gps[1])
        load_T(lnb[:], P["conv_ln_b"][l].rr("(t p) -> t p", p=128), 2, gps[2])
        for t in range(2):
            load_T(cdw[:, t, :], P["conv_dw"][l, :, t * 128:(t + 1) * 128], 31, gps[t])
            load_T(scw[:, t, :], P["sconv_w"][l, :, t * 128:(t + 1) * 128], 3, gps[2])
        for t in range(2):
            fw.dma(stgo[t][:, 0:256], P["conv_pw"][l, t * 128:(t + 1) * 128, :])
            fw.copy(dve, pw[:, t, :], stgo[t][:, 0:256])
            for k in range(31):
                fw.ts(dve if k % 2 == 0 else pool, diag[:, t, k, :], ident[:], cdw[:, t, k:k + 1], None, ALU.mult)
        fw.dma(lamt[:], P["diff_lambda"][l].m(lambda a: a.unsqueeze(0).broadcast_to([64, 4, 32])))
        fw.tt(dve, lamp[:], lamt[:, 0:4:2, :], lamt[:, 1:4:2, :], ALU.mult)
        fw.reduce(lams[:], lamp[:], ALU.add)
        fw.actv(lams[:], lams[:], AF.Exp)
        fw.tt(dve, neglam[:], lams[:, 1:2], lams[:, 0:1], ALU.subtract)
        fw.ts(dve, neglam[:], neglam[:], -lam_init, None, ALU.add)
        fw.dma(gsub[:], P["diff_subln"][l].rr("(d o) -> d o", o=1))
        fw.ts(dve, gsub[:], gsub[:], 1.0 - lam_init, None, ALU.mult)

        def group_norm(ys, cidx, Pn):
            n = len(ys)
            gp = gps[0]
            for i, y in enumerate(ys):
                fw.actv(sq[i][0:Pn, :], y, AF.Square)
            for i in range(n):
                fw.mm(gp[0:Pn, :], ones[0:Pn, 0:Pn], sq[i][0:Pn, :], start=(i == 0), stop=(i == n - 1))
            fw.rsqrt(rs[0:Pn, :], gp[0:Pn, :], 1.0 / GW, EPS)
            for i, y in enumerate(ys):
                fw.stt(catT[0:Pn, cidx[i], :], y, gn[0:Pn, cidx[i]:cidx[i] + 1], rs[0:Pn, :], ALU.mult, ALU.mult)

        def attn_head(qa, qb, kt_tile, r0, nd, nkt, Vt, h, qc, dm, out_sb):
            op_ = ops[attn_head.n % 2]
            attn_head.n += 1
            na = nd + (4 if dm is not None else 0)
            ra = slice(r0, r0 + na)
            rdd = slice(r0, r0 + nd)
            def qk(kt):
                sp_ = sps[kt % 3]
                kc_ = slice(kt * 128, (kt + 1) * 128)
                rel = kt - 4 * qc
                if dm is None or rel < 0:
                    fw.mm(sp_[:], kt_tile[ra, kc_], qa[ra, :])
                elif rel > 3:
                    fw.mm(sp_[:], kt_tile[ra, kc_], qb[ra, :])
                else:
                    if rel > 0:
                        fw.mm(sp_[:, 0:rel * 128], kt_tile[ra, kc_], qb[ra, 0:rel * 128], signal=False)
                    dc = slice(rel * 128, (rel + 1) * 128)
                    fw.mm(sp_[:, dc], kt_tile[rdd, kc_], qa[rdd, dc], start=True, stop=False, signal=False)
                    fw.mm(sp_[:, dc], ident[:], dm, start=False, stop=True, signal=(rel == 3))
                    if rel < 3:
                        fw.mm(sp_[:, (rel + 1) * 128:512], kt_tile[ra, kc_], qa[ra, (rel + 1) * 128:512])
                fw.actv(pT[kt % 3][:], sp_[:], AF.Exp)

            qk(0)
            for kt in range(nkt):
                if kt + 1 < nkt:
                    qk(kt + 1)
                fw.mm(op_[0:65, :], Vt[:, kt, h, 0:65], pT[kt % 3][:], start=(kt == 0), stop=(kt == nkt - 1))
            fw.recip(rd[64:65, :], op_[64:65, :])
            bp = gps[1]
            fw.mm(bp[0:64, :], onesf[64:65, 0:64], rd[64:65, :])
            fw.copy(act, bcs[:], bp[0:64, :])
            fw.tt(dve, out_sb, op_[0:64, :], bcs[:], ALU.mult)

        attn_head.n = 0

        for c in range(NCH):
            c0 = c * 512
            for hp in range(4):
                fw.dma(qA[hp][:], qTd[hp, :, c0:c0 + 512])
                fw.dma(qB[hp][:], qTd[hp, :, c0:c0 + 512])
                for side in range(2):
                    h = 2 * (hp % 2) + side
                    r = slice(side * 64 + 32, side * 64 + 36)
                    fw.dma(qA[hp][r, :], C["c_aq"][h, :, c0:c0 + 512])
                    fw.dma(qB[hp][r, :], C["c_aqn"][h, :, c0:c0 + 512])
            for hp in range(2):
                fw.dma(qm[hp][:], qmTd[hp, :, c0:c0 + 512])

            for h in range(4):
                for j in range(2):
                    hp = j * 2 + h // 2
                    side = h % 2
                    attn_head(qA[hp], qB[hp], kT[hp], side * 64, 32, NKT, Vaug, h, c, dmat[:, h, :], oj[j][:])
                fw.stt(yh[h][:], oj[1][:], neglam[:], oj[0][:], ALU.mult, ALU.add)
                fw.actv(sq[0][0:64, :], yh[h][:], AF.Square)
                fw.mm(gps[2][0:64, :], ones[0:64, 0:64], sq[0][0:64, :])
                fw.rsqrt(rs[0:64, :], gps[2][0:64, :], 1.0 / 64, EPS)
                fw.stt(yh[h][:], yh[h][:], gsub[:], rs[0:64, :], ALU.mult, ALU.mult)
            group_norm([yh[h][:] for h in range(4)], [4, 5, 6, 7], 64)

            for h in range(4):
                hp, side = h // 2, h % 2
                attn_head(qm[hp], qm[hp], kmT[hp], side * 64, 64, 2, Vm, h, c, None, yh[h][:])
            group_norm([yh[h][:] for h in range(4)], [10, 11, 12, 13], 64)

            lo = max(c0 - 8, 0)
            hi_ = min(c0 + 520, S)
            ys = []
            for t in range(2):
                u = f1[t]
                if c == 0 or c == NCH - 1:
                    fw.memset(dve, u[:, 0:528], 0.0)
                fw.dma(u[:, lo - (c0 - 8):hi_ - (c0 - 8)], projT[t, :, lo:hi_])
                s2, s4, s8, s16 = f1[2], f1[3], f1[4], f1[5]
                fw.tt(dve, s2[:, 1:528], u[:, 0:527], u[:, 1:528], ALU.add)
                fw.tt(dve, s4[:, 2:527], s2[:, 1:526], s2[:, 3:528], ALU.add)
                fw.tt(dve, s8[:, 4:525], s4[:, 2:523], s4[:, 6:527], ALU.add)
                fw.tt(dve, s16[:, 8:520], s8[:, 4:516], s8[:, 12:524], ALU.add)
                srcs = (s2, s4) if t == 0 else (s8, s16)
                for g2 in range(2):
                    pr = slice(g2 * 64, g2 * 64 + 64)
                    sw = srcs[g2]
                    if c == 0:
                        fw.tt(dve, sw[pr, 8:16], sw[pr, 8:16], edge[pr, t, 0:8], ALU.mult)
                    if c == NCH - 1:
                        fw.tt(dve, sw[pr, 512:520], sw[pr, 512:520], edge[pr, t, 8:16], ALU.mult)
                    fw.stt(pb[t][pr, :], sw[pr, 8:520], invw[pr, t, :], u[pr, 8:520], ALU.mult, ALU.subtract)
                fw.mm(gps[2][:], BD[:, t, :], pb[t][:])
                fw.actv(yb[t][:], gps[2][:], AF.Copy, scale=pscale[:, t:t + 1])
                ys.append(yb[t][:])
            group_norm(ys, [0, 1], 128)

            lo = max(c0 - 1, 0)
            hi_ = min(c0 + 513, S)
            ys = []
            for t in range(2):
                hh, bb, cc, z, cv = f1[0], f1[1], f1[2], f1[3], f1[4]
                if c == 0 or c == NCH - 1:
                    fw.memset(dve, hh[:, 0:514], 0.0)
                    fw.memset(dve, cc[:, 0:514], 0.0)
                fw.dma(hh[:, lo - (c0 - 1):hi_ - (c0 - 1)], projT[6 + t, :, lo:hi_])
                fw.dma(bb[:, 0:512], projT[8 + t, :, c0:c0 + 512])
                fw.dma(cc[:, lo - (c0 - 1):hi_ - (c0 - 1)], projT[10 + t, :, lo:hi_])
                fw.tt(dve, z[:, 0:514], hh[:, 0:514], cc[:, 0:514], ALU.mult)
                fw.ts(dve, cv[:, 0:512], z[:, 0:512], scw[:, t, 0:1], None, ALU.mult)
                fw.stt(cv[:, 0:512], z[:, 1:513], scw[:, t, 1:2], cv[:, 0:512], ALU.mult, ALU.add)
                fw.stt(cv[:, 0:512], z[:, 2:514], scw[:, t, 2:3], cv[:, 0:512], ALU.mult, ALU.add)
                fw.tt(dve, yb[t][:], bb[:, 0:512], cv[:, 0:512], ALU.mult)
                ys.append(yb[t][:])
            group_norm(ys, [8, 9], 128)

            lo = max(c0 - 15, 0)
            hi_ = min(c0 + 527, S)
            cvs = []
            for t in range(2):
                a_, g_ = f2[t * 2], f2[t * 2 + 1]
                if c == 0 or c == NCH - 1:
                    fw.memset(dve, a_[:, 0:542], 0.0)
                    fw.memset(dve, g_[:, 0:542], 0.0)
                fw.dma(a_[:, lo - (c0 - 15):hi_ - (c0 - 15)], projT[2 + t, :, lo:hi_])
                fw.dma(g_[:, lo - (c0 - 15):hi_ - (c0 - 15)], projT[4 + t, :, lo:hi_])
                fw.actv(g_[:, 0:542], g_[:, 0:542], AF.Sigmoid)
                fw.tt(dve, hb[t][:, 0:542], a_[:, 0:542], g_[:, 0:542], ALU.mult)
                cp = sps[t]
                for k in range(31):
                    fw.mm(cp[:], diag[:, t, k, :], hb[t][:, k:k + 512], start=(k == 0), stop=(k == 30))
                cvs.append(cp)
            for t in range(2):
                fw.copy(act, pb[t][:], cvs[t][:])
                fw.actv(sq[t][:], cvs[t][:], AF.Square)
            for t in range(2):
                fw.mm(gps[0][:], ones[:], pb[t][:], start=(t == 0), stop=(t == 1))
            for t in range(2):
                fw.mm(gps[1][:], ones[:], sq[t][:], start=(t == 0), stop=(t == 1))
            mu, var = f1[0], f1[1]
            fw.ts(dve, mu[:, 0:512], gps[0][:], 1.0 / GW, None, ALU.mult)
            fw.tt(dve, var[:, 0:512], mu[:, 0:512], mu[:, 0:512], ALU.mult)
            fw.stt(var[:, 0:512], gps[1][:], 1.0 / GW, var[:, 0:512], ALU.mult, ALU.subtract)
            fw.ts(dve, var[:, 0:512], var[:, 0:512], 0.0, None, ALU.max)
            fw.rsqrt(var[:, 0:512], var[:, 0:512], 1.0, EPS)
            for t in range(2):
                xc = f1[2 + t]
                fw.tt(dve, xc[:, 0:512], cvs[t][:], mu[:, 0:512], ALU.subtract)
                fw.tt(dve, xc[:, 0:512], xc[:, 0:512], var[:, 0:512], ALU.mult)
                fw.actv(hb[t][:, 0:512], xc[:, 0:512], AF.Silu, bias=lnb[:, t:t + 1], scale=lng[:, t:t + 1])
            ys = []
            for dt_ in range(2):
                yp = gps[2]
                for t in range(2):
                    fw.mm(yp[:], pw[:, t, dt_ * 128:(dt_ + 1) * 128], hb[t][:, 0:512], start=(t == 0), stop=(t == 1))
                fw.copy(act, yb[dt_][:], yp[:])
                ys.append(yb[dt_][:])
            group_norm(ys, [2, 3], 128)

            for sub in range(4):
                xx = xr[sub % 2]
                r0 = c0 + sub * 128
                fw.dma(xx[:], cur_x[r0:r0 + 128, :])
                for dh in range(2):
                    op_ = ops[dh]
                    for ci, (_, n) in enumerate(wo_rows):
                        fw.mm(op_[:], catT[0:n, ci, sub * 128:(sub + 1) * 128], Wo[0:n, ci, dh * 512:(dh + 1) * 512],
                              start=(ci == 0), stop=(ci == 13))
                    fw.tt(dve, xx[:, dh * 512:(dh + 1) * 512], xx[:, dh * 512:(dh + 1) * 512], op_[:], ALU.add)
                dstm = xmid if with_peer else dst_x
                fw.dma(dstm.k(("x", c, sub))[r0:r0 + 128, :], xx[:], q=stq)
        fw.pop()
        fw.pop()

        if with_peer:
            peer_phase(fw, nc, S, l, P, xmid, dst_x, ident, identf, iota, uT_d, v_d)
        cur_x = dst_x

    fw.barrier()
    fw.close()
    return nc, cs


def peer_phase(fw, nc, S, l, P, xmid, dst_x, ident, identf, iota, uT_d, v_d):
    import os
    STAGE = int(os.environ.get("PEER_STAGE", "9"))
    dve, act, pool, pe = fw.dve, fw.act, fw.pool, fw.pe
    stq = fw.act
    TCH = 256
    NTC = S // TCH
    NEG = -1e30

    fw.push()
    uf = [fw.sbuf(f"uf{i}", [128, D], F32) for i in range(2)]
    ub = [fw.sbuf(f"ub{i}", [128, 8, 128], BF16) for i in range(2)]
    vf = [fw.sbuf(f"vf{i}", [128, D], F32) for i in range(2)]
    vb = [fw.sbuf(f"vb{i}", [128, D], BF16) for i in range(2)]
    pt = [fw.psum(f"pt{i}", [128, 8, 128], F32) for i in range(2)]
    for c in range(128):
        i = c % 2
        fw.dma(uf[i][:], P["peer_u"][l, c * 128:(c + 1) * 128, :])
        for kc in range(8):
            fw.tr(pt[i][:, kc, :], uf[i][:, kc * 128:(kc + 1) * 128], identf[:], signal=(kc == 7))
        fw.copy(act, ub[i][:], pt[i][:])
        fw.dma(uT_d.k(("u", c))[c], ub[i][:], q=stq)
        fw.dma(vf[i][:], P["peer_v"][l, c * 128:(c + 1) * 128, :])
        fw.copy(pool, vb[i][:], vf[i][:])
        fw.dma(v_d.k(("v", c))[c * 128:(c + 1) * 128, :], vb[i][:], q=stq)
    fw.pop()
    if STAGE <= 1:
        fw.dma(dst_x[:], xmid[:])
        return

    fw.push()
    Wq = fw.sbuf("Wq", [128, 8, 2048], BF16)
    skT = fw.sbuf("skT", [128, 16, 128], BF16)
    gffn = fw.sbuf("gffn", [128, 8], F32)
    ldst = fw.sbuf("ldst", [8, 128], F32)
    bufA = fw.sbuf("bufA", [128, 2048], F32)
    bufB = fw.sbuf("bufB", [128, 2048], F32)
    bufC = fw.sbuf("bufC", [128, 2048], F32)
    xt2 = [fw.sbuf(f"xt2_{i}", [128, D], F32) for i in range(2)]
    xs2 = fw.sbuf("xs2", [128, D], BF16)
    ssv = fw.sbuf("ssv2", [128, 1], F32)
    xn2T = fw.sbuf("xn2T", [128, 8, TCH], BF16)
    qT = fw.sbuf("qT", [128, 16, TCH], BF16)
    a16 = fw.sbuf("a16", [128, 256], F32)
    ia = fw.sbuf("ia", [128, 256], U32)
    iaf = fw.sbuf("iaf", [128, 256], F32)
    best = fw.sbuf("best", [128, 128], F32)
    ci = fw.sbuf("ci", [128, 128], U32)
    eb = fw.sbuf("eb", [128, 128], F32)
    zs = fw.sbuf("zs", [128, 8], F32)
    gg = fw.sbuf("gg", [128, 128], F32)
    r1i = fw.sbuf("r1i", [128, 128], U32)
    r2i = fw.sbuf("r2i", [128, 128], U32)
    r1f = fw.sbuf("r1f", [128, 128], F32)
    r2f = fw.sbuf("r2f", [128, 128], F32)
    i1s = fw.sbuf("i1s", [128, 128], F32)
    i2s = fw.sbuf("i2s", [128, 128], F32)
    tT = fw.sbuf("tT", [128, 3, 128], F32)
    OHA = fw.sbuf("OHA", [128, 32, 128], BF16)
    OHB = fw.sbuf("OHB", [128, 32, 128], BF16)
    GT = fw.sbuf("GT", [128, 128, TCH], BF16)
    iota_bf = fw.sbuf("iota_bf", [128, 128], BF16)
    fw.copy(dve, iota_bf[:], iota[:])
    utb = [fw.sbuf(f"utb{i}", [128, 2, 8, 128], BF16) for i in range(2)]
    vtb = [fw.sbuf(f"vtb{i}", [128, 2, D], BF16) for i in range(2)]
    geb = [fw.sbuf(f"geb{i}", [128, TCH], BF16) for i in range(2)]
    wt = [fw.sbuf(f"wt{i}", [128, TCH], BF16) for i in range(2)]
    oacc = [fw.psum(f"oacc{i}", [128, D], F32) for i in range(2)]
    pbig = fw.psum("pbig", [128, 2048], F32)

    def bank(i):
        return pbig.k(("b", i))[:, i * 512:(i + 1) * 512]

    fw.dma(ldst[:], P["norm_ffn"][l].rr("(kc kp) -> kc kp", kp=128))
    fw.tr(bank(0)[:, 0:8], ldst[:], identf[0:8, 0:8])
    fw.copy(dve, gffn[:], bank(0)[:, 0:8])
    for kc in range(8):
        sg = bufA if kc % 2 == 0 else bufB
        fw.dma(sg[:], P["peer_wq"][l, kc * 128:(kc + 1) * 128, :])
        fw.copy(pool if kc % 2 == 0 else act, Wq[:, kc, :], sg[:])
    for hj in range(16):
        sg = bufC
        fw.dma(sg.k(hj % 4)[:, (hj % 4) * 128:(hj % 4 + 1) * 128], P["peer_subkeys"][l, hj // 2, hj % 2])
        b = bank(hj % 4)
        fw.tr(b[:, 0:128], sg.k(hj % 4)[:, (hj % 4) * 128:(hj % 4 + 1) * 128], identf[:])
        fw.copy(act if hj % 2 == 0 else dve, skT[:, hj, :], b[:, 0:128])

    if STAGE <= 2:
        fw.dma(dst_x[:], xmid[:])
        fw.pop()
        return
    a16v = a16[:].rr("p (h j r) -> p h j r", h=8, j=2)
    iafv = iaf[:].rr("p (h j r) -> p h j r", h=8, j=2)

    for tc in range(NTC):
        t0 = tc * TCH
        for sub in range(2):
            r0 = t0 + sub * 128
            fw.dma(xt2[sub][:], xmid[r0:r0 + 128, :])
            fw.actv(bufA[:, 0:D], xt2[sub][:], AF.Square, accum=ssv[:])
            fw.rsqrt(ssv[:], ssv[:], 1.0 / D, EPS)
            fw.actv(xs2[:], xt2[sub][:], AF.Copy, scale=ssv[:])
            tpb = bank(sub).m(lambda a: a.bitcast(BF16).rearrange("p (k t) -> p k t", k=8))
            for kc in range(8):
                fw.tr(tpb[:, kc, :], xs2[:, kc * 128:(kc + 1) * 128], ident[:], signal=(kc == 7))
            fw.tt(dve, xn2T[:, :, sub * 128:(sub + 1) * 128], tpb,
                  gffn[:].m(lambda a: a.unsqueeze(2).broadcast_to([128, 8, 128])), ALU.mult)
        for hj in range(16):
            b = bank(hj % 4)
            for kc in range(8):
                fw.mm(b[:, 0:TCH], Wq[:, kc, hj * 128:(hj + 1) * 128], xn2T[:, kc, :], start=(kc == 0), stop=(kc == 7))
            fw.copy(act if hj % 2 == 0 else dve, qT[:, hj, :], b[:, 0:TCH])
        for sub in range(2 if STAGE >= 4 else 0):
            tsl = slice(sub * 128, (sub + 1) * 128)
            for hj in range(16):
                fw.mm(pbig.k(("b", hj // 4))[:, hj * 128:(hj + 1) * 128], qT[:, hj, tsl], skT[:, hj, :])
            for i in range(4):
                fw.copy(act if i % 2 == 0 else dve, bufA.k(i)[:, i * 512:(i + 1) * 512], bank(i))
            def _g1(hj):
                sl = slice(hj * 128, (hj + 1) * 128)
                return (bufA.k(hj // 4)[:, sl], bufB.k(hj)[:, sl], a16.k(hj)[:, hj * 16:hj * 16 + 8], a16.k(hj)[:, hj * 16 + 8:hj * 16 + 16],
                        ia.k(hj)[:, hj * 16:hj * 16 + 8], ia.k(hj)[:, hj * 16 + 8:hj * 16 + 16])
            G1 = [_g1(hj) for hj in range(16)]
            for (sA, sB, a0, a1, i0, i1_) in G1:
                fw.op(dve, lambda: nc.vector.max(a0.ap, sA.ap), [sA], [a0])
            for (sA, sB, a0, a1, i0, i1_) in G1:
                fw.op(dve, lambda: nc.vector.match_replace(sB.ap, a0.ap, sA.ap, NEG), [sA, a0], [sB])
            for (sA, sB, a0, a1, i0, i1_) in G1:
                fw.op(dve, lambda: nc.vector.max_index(i0.ap, a0.ap, sA.ap), [sA, a0], [i0])
            for (sA, sB, a0, a1, i0, i1_) in G1:
                fw.op(dve, lambda: nc.vector.max(a1.ap, sB.ap), [sB], [a1])
            for (sA, sB, a0, a1, i0, i1_) in G1:
                fw.op(dve, lambda: nc.vector.max_index(i1_.ap, a1.ap, sB.ap), [sB, a1], [i1_])
            cand = bufB[:].rr("p (h a b) -> p h a b", h=8, a=16)
            fw.tt(dve, cand, a16v[:, :, 0, :].m(lambda a: a.unsqueeze(3).broadcast_to([128, 8, 16, 16])),
                  a16v[:, :, 1, :].m(lambda a: a.unsqueeze(2).broadcast_to([128, 8, 16, 16])), ALU.add)
            def _g2(h):
                sl = slice(h * 256, (h + 1) * 256)
                return (bufB.k(("c", h))[:, sl], bufC.k(("c", h))[:, sl], best.k(h)[:, h * 16:h * 16 + 8], best.k(h)[:, h * 16 + 8:h * 16 + 16],
                        ci.k(h)[:, h * 16:h * 16 + 8], ci.k(h)[:, h * 16 + 8:h * 16 + 16])
            G2 = [_g2(h) for h in range(8)]
            for (sA, sB, a0, a1, i0, i1_) in G2:
                fw.op(dve, lambda: nc.vector.max(a0.ap, sA.ap), [sA], [a0])
            for (sA, sB, a0, a1, i0, i1_) in G2:
                fw.op(dve, lambda: nc.vector.match_replace(sB.ap, a0.ap, sA.ap, NEG), [sA, a0], [sB])
            for (sA, sB, a0, a1, i0, i1_) in G2:
                fw.op(dve, lambda: nc.vector.max_index(i0.ap, a0.ap, sA.ap), [sA, a0], [i0])
            for (sA, sB, a0, a1, i0, i1_) in G2:
                fw.op(dve, lambda: nc.vector.max(a1.ap, sB.ap), [sB], [a1])
            for (sA, sB, a0, a1, i0, i1_) in G2:
                fw.op(dve, lambda: nc.vector.max_index(i1_.ap, a1.ap, sB.ap), [sB, a1], [i1_])
            bv = best[:].rr("p (h s) -> p h s", h=8)
            ev_ = eb[:].rr("p (h s) -> p h s", h=8)
            fw.tt(dve, ev_, bv, bv[:, :, 0:1].m(lambda a: a.broadcast_to([128, 8, 16])), ALU.subtract)
            fw.actv(eb[:], eb[:], AF.Exp)
            fw.reduce(zs[:], ev_, ALU.add)
            fw.recip(zs[:], zs[:])
            fw.tt(dve, gg[:].rr("p (h s) -> p h s", h=8), ev_, zs[:].m(lambda a: a.unsqueeze(2).broadcast_to([128, 8, 16])), ALU.mult)
            fw.copy(dve, iaf[:], ia[:])
            fw.ts(dve, r1i[:], ci[:], 4, None, ALU.logical_shift_right)
            fw.ts(dve, r2i[:], ci[:], 15, None, ALU.bitwise_and)
            fw.copy(dve, r1f[:], r1i[:])
            fw.copy(dve, r2f[:], r2i[:])
            for (rf, jj, dst) in ((r1f, 0, i1s), (r2f, 1, i2s)):
                oh = bufA[:].rr("p (a r) -> p a r", r=16)
                fw.tt(dve, oh, iota[:, 0:16].m(lambda a: a.unsqueeze(1).broadcast_to([128, 128, 16])),
                      rf[:].m(lambda a: a.unsqueeze(2).broadcast_to([128, 128, 16])), ALU.is_equal)
                oh4 = bufA[:].rr("p (h s r) -> p h s r", h=8, s=16)
                fw.tt(dve, oh4, oh4, iafv[:, :, jj, :].m(lambda a: a.unsqueeze(2).broadcast_to([128, 8, 16, 16])), ALU.mult)
                fw.reduce(dst[:], oh, ALU.add)
            b = bank(0)
            for i, src in enumerate((i1s, i2s, gg)):
                fw.tr(b[:, i * 128:(i + 1) * 128], src[:], identf[:], signal=(i == 2))
            fw.copy(act, tT[:], b[:, 0:384].rr("p (a t) -> p a t", a=3))
            for g4 in range(4):
                ts_ = slice(g4 * 32, (g4 + 1) * 32)
                for tk in range(32):
                    tcol = g4 * 32 + tk
                    fw.ts(dve, OHA.k(tk)[:, tk, :], iota_bf[:], tT[:, 0, tcol:tcol + 1], tT[:, 2, tcol:tcol + 1], ALU.is_equal, ALU.mult)
                    fw.ts(dve, OHB.k(tk)[:, tk, :], iota_bf[:], tT[:, 1, tcol:tcol + 1], None, ALU.is_equal)
                for q4 in range(8):
                    b = bank(1 + q4 % 2)
                    for k in range(4):
                        tk = q4 * 4 + k
                        fw.mm(b[:, k * 128:(k + 1) * 128], OHB.k(tk)[:, tk, :], OHA.k(tk)[:, tk, :], signal=(k == 3))
                    tg = sub * 128 + g4 * 32 + q4 * 4
                    fw.copy(act, GT[:, :, tg:tg + 4].rr("p c t -> p t c"),
                            b[:].rr("p (t c) -> p t c", t=4))
        def dense_U(c):
            cg, cc = c // 2, c % 2
            i = cg % 2
            if cc == 0:
                fw.dma(utb[i][:], uT_d[cg * 2:(cg + 1) * 2].rr("c p k e -> p c k e"))
                fw.dma(vtb[i][:], v_d[cg * 256:(cg + 1) * 256, :].rr("(c p) d -> p c d", p=128))
            b = bank(c % 2 + 2)
            for kc in range(8):
                fw.mm(b[:, 0:TCH], utb[i][:, cc, kc, :], xn2T[:, kc, :], start=(kc == 0), stop=(kc == 7))
            fw.actv(geb[c % 2][:], b[:, 0:TCH], AF.Gelu)
            fw.tt(dve, wt[c % 2][:], geb[c % 2][:], GT[:, c, :], ALU.mult)

        def dense_V(c):
            cg, cc = c // 2, c % 2
            i = cg % 2
            for sub in range(2):
                for dh in range(2):
                    fw.mm(oacc[sub].k(dh)[:, dh * 512:(dh + 1) * 512], wt[c % 2][:, sub * 128:(sub + 1) * 128],
                          vtb[i][:, cc, dh * 512:(dh + 1) * 512], start=(c == 0), stop=(c == 127),
                          signal=(sub == 1 and dh == 1))

        if STAGE >= 5:
            dense_U(0)
            for c in range(128):
                if c + 1 < 128:
                    dense_U(c + 1)
                dense_V(c)
        for sub in range(2):
            r0 = t0 + sub * 128
            for dh in range(2 if STAGE >= 5 else 0):
                fw.tt(dve, xt2[sub][:, dh * 512:(dh + 1) * 512], xt2[sub][:, dh * 512:(dh + 1) * 512],
                      oacc[sub].k(dh)[:, dh * 512:(dh + 1) * 512], ALU.add)
            fw.dma(dst_x.k(("o", tc, sub))[r0:r0 + 128, :], xt2[sub][:], q=stq)
    fw.pop()


_CACHE = {}


def kernel(**inputs):
    n = 8
    if "nc" not in _CACHE:
        _CACHE["nc"] = build()
    nc, cs = _CACHE["nc"]
    in_maps = []
    for b in range(n):
        m = {"x": np.ascontiguousarray(inputs["x"][b]), "mem": np.ascontiguousarray(inputs["mem"][b])}
        for k in PARAM_SHAPES:
            m[k] = np.ascontiguousarray(inputs[k])
        m.update(cs)
        in_maps.append(m)
    res = run_bass_kernel_spmd(nc, in_maps, core_ids=list(range(n)))
    return np.stack([r["out"] for r in res.results], axis=0).astype(np.float32)
```

```python
import math
import numpy as np
import ml_dtypes
import concourse.bass as bass
import concourse.mybir as mybir
from concourse.bass_utils import run_bass_kernel_spmd

F32 = mybir.dt.float32
BF16 = mybir.dt.bfloat16
I32 = mybir.dt.int32
U32 = mybir.dt.uint32
AF = mybir.ActivationFunctionType
ALU = mybir.AluOpType
AX = mybir.AxisListType

D = 1024
GW = 256
EPS = 1e-6
NMEM = 256
SEQ = 4096
DEPTH = 4
NKEYS = 128
NEXP = 16384
import os as _os0
RSQRT_ACT = _os0.environ.get("RSQRT_ACT", "1") == "1"
import os as _os
NO_SELF_SYNC = tuple(x for x in _os.environ.get("NO_SELF_SYNC", "").split(",") if x)


class V:
    def __init__(self, ap, t, key=None):
        self.ap = ap
        self.t = t
        self.key = key

    def __getitem__(self, idx):
        return V(self.ap[idx], self.t, self.key)

    def rr(self, pat, **kw):
        return V(self.ap.rearrange(pat, **kw), self.t, self.key)

    def bc(self, shape):
        return V(self.ap.broadcast_to(list(shape)), self.t, self.key)

    def m(self, f):
        return V(f(self.ap), self.t, self.key)


class T:
    def __init__(self, root, name, is_ap=False):
        self.root = root
        self.name = name
        self.st = {}

    def __getitem__(self, idx):
        return V(self.root[idx], self, None)

    def k(self, key):
        t = self

        class _K:
            def __getitem__(self, idx):
                return V(t.root[idx], t, key)

        return _K()


class Eng:
    def __init__(self, fw, name, hw):
        self.fw = fw
        self.name = name
        self.hw = hw
        self.waited = {}
        self.sem = fw.new_sem(name)
        self.count = 0

    def wait(self, ev, force=False):
        if ev is None:
            return
        sem, val = ev
        k = id(sem)
        if sem is self.sem and (val > self.count or self.name == "pe"):
            return
        if sem is self.sem and not force and self.name in NO_SELF_SYNC:
            return
        if self.waited.get(k, 0) >= val:
            return
        self.hw.wait_ge(sem, val)
        self.waited[k] = val
        self.fw.n_waits += 1


class FW:
    def __init__(self, nc, n_dma_sems=32):
        self.nc = nc
        self._stack = []
        self._scopes = []
        self.n_waits = 0
        self.n_inst = 0
        self.tiles = []
        self.pe = Eng(self, "pe", nc.tensor)
        self.act = Eng(self, "act", nc.scalar)
        self.dve = Eng(self, "dve", nc.vector)
        self.pool = Eng(self, "pool", nc.gpsimd)
        self.sp = Eng(self, "sp", nc.sync)
        self.engs = [self.pe, self.act, self.dve, self.pool, self.sp]
        self.dma_pools = {
            "sp": [[[self.new_sem(f"dsp{i}"), 0] for i in range(24)], 0],
            "act": [[[self.new_sem(f"dac{i}"), 0] for i in range(8)], 0],
        }

    def new_sem(self, name):
        cm = self.nc.semaphore(name)
        s = cm.__enter__()
        self._stack.append(cm)
        return s

    def _alloc(self, cm, name):
        t = cm.__enter__()
        if self._scopes:
            self._scopes[-1].append(cm)
        else:
            self._stack.append(cm)
        tt = T(t, name)
        self.tiles.append(tt)
        return tt

    def _uname(self, name):
        self.uid = getattr(self, "uid", 0) + 1
        return f"{name}_{self.uid}"

    def sbuf(self, name, shape, dtype):
        name = self._uname(name)
        return self._alloc(self.nc.sbuf_tensor(name, list(shape), dtype), name)

    def psum(self, name, shape, dtype=F32):
        name = self._uname(name)
        return self._alloc(self.nc.psum_tensor(name, list(shape), dtype), name)

    def dram(self, name, shape, dtype, kind="Internal"):
        t = self.nc.dram_tensor(name, list(shape), dtype, kind=kind)
        tt = T(t.ap(), name)
        self.tiles.append(tt)
        return tt

    def push(self):
        self._scopes.append([])

    def pop(self):
        self.barrier()
        for cm in reversed(self._scopes.pop()):
            cm.__exit__(None, None, None)

    def close(self):
        while self._stack:
            self._stack.pop().__exit__(None, None, None)

    def barrier(self):
        evs = [(e.sem, e.count) for e in self.engs if e.count > 0]
        for pl, _ in self.dma_pools.values():
            evs += [(s, v) for s, v in pl if v > 0]
        for e in self.engs:
            for ev in evs:
                e.wait(ev)
        for t in self.tiles:
            t.st = {}

    def _deps(self, eng, reads, writes, force=False):
        for v in reads:
            for k, st in v.t.st.items():
                if v.key is None or k is None or k == v.key:
                    eng.wait(st[0], force)
        for v in writes:
            for k, st in v.t.st.items():
                if v.key is None or k is None or k == v.key:
                    eng.wait(st[0], force)
                    for ev in st[1].values():
                        eng.wait(ev, force)

    def _mark(self, ev, reads, writes):
        sem, val = ev
        for v in reads:
            st = v.t.st.setdefault(v.key, [None, {}])
            st[1][id(sem)] = ev
        for v in writes:
            if v.key is None:
                v.t.st = {None: [ev, {}]}
            else:
                v.t.st[v.key] = [ev, {}]

    def op(self, eng, fn, reads=(), writes=(), signal=True):
        reads = [r for r in reads if isinstance(r, V)]
        self._deps(eng, reads, writes)
        ins = fn()
        self.n_inst += 1
        if signal:
            ins.then_inc(eng.sem, 1)
            eng.count += 1
            ev = (eng.sem, eng.count)
        else:
            ev = (eng.sem, eng.count + 1)
        self._mark(ev, reads, writes)
        return ev

    def dma(self, out, in_, q=None, **kw):
        q = q or self.sp
        self._deps(q, [in_], [out], force=True)
        pl = self.dma_pools[q.name]
        slot = pl[0][pl[1]]
        pl[1] = (pl[1] + 1) % len(pl[0])
        sem, val = slot
        if val > 0:
            q.wait((sem, val))
        ins = q.hw.dma_start(out=out.ap, in_=in_.ap, **kw)
        ins.then_inc(sem, 16)
        slot[1] = val + 16
        ev = (sem, val + 16)
        self.n_inst += 1
        self._mark(ev, [in_], [out])
        return ev

    @staticmethod
    def _a(x):
        return x.ap if isinstance(x, V) else x

    def mm(self, out, lhsT, rhs, start=True, stop=True, signal=None):
        if signal is None:
            signal = stop
        return self.op(self.pe, lambda: self.nc.tensor.matmul(out.ap, lhsT=lhsT.ap, rhs=rhs.ap, start=start, stop=stop),
                       [lhsT, rhs], [out], signal=signal)

    def tr(self, out, in_, ident, signal=True):
        return self.op(self.pe, lambda: self.nc.tensor.transpose(out.ap, in_.ap, ident.ap), [in_, ident], [out], signal=signal)

    def actv(self, out, in_, func, bias=None, scale=None, accum=None):
        kw = {}
        if bias is not None:
            kw["bias"] = self._a(bias)
        if scale is not None:
            kw["scale"] = self._a(scale)
        w = [out]
        if accum is not None:
            kw["accum_out"] = accum.ap
            w.append(accum)
        return self.op(self.act, lambda: self.nc.scalar.activation(out.ap, in_.ap, func, **kw), [in_, bias, scale], w)

    def _ve(self, eng):
        return self.nc.vector if eng is self.dve else self.nc.gpsimd

    def ts(self, eng, out, in0, s1, s2, op0, op1=None, accum=None):
        kw = {}
        if op1 is not None:
            kw["op1"] = op1
        w = [out]
        if accum is not None:
            kw["accum_out"] = accum.ap
            w.append(accum)
        return self.op(eng, lambda: self._ve(eng).tensor_scalar(out.ap, in0.ap, self._a(s1), self._a(s2), op0, **kw),
                       [in0, s1, s2], w)

    def tt(self, eng, out, in0, in1, op):
        return self.op(eng, lambda: self._ve(eng).tensor_tensor(out.ap, in0.ap, in1.ap, op), [in0, in1], [out])

    def stt(self, out, in0, scalar, in1, op0, op1):
        return self.op(self.dve, lambda: self.nc.vector.scalar_tensor_tensor(out.ap, in0.ap, self._a(scalar), in1.ap, op0, op1),
                       [in0, scalar, in1], [out])

    def copy(self, eng, out, in_):
        if eng is self.act:
            return self.op(eng, lambda: self.nc.scalar.copy(out.ap, in_.ap), [in_], [out])
        return self.op(eng, lambda: self._ve(eng).tensor_copy(out.ap, in_.ap), [in_], [out])

    def recip(self, out, in_):
        return self.op(self.dve, lambda: self.nc.vector.reciprocal(out.ap, in_.ap), [in_], [out])

    def reduce(self, out, in_, op, axis=AX.X):
        return self.op(self.dve, lambda: self.nc.vector.tensor_reduce(out.ap, in_.ap, axis, op), [in_], [out])

    def memset(self, eng, out, val):
        return self.op(eng, lambda: self._ve(eng).memset(out.ap, val), [], [out])

    def rsqrt(self, out, in_, scale, eps):
        if RSQRT_ACT:
            self.actv(out, in_, AF.Ln, bias=eps, scale=scale)
            self.actv(out, out, AF.Exp, scale=-0.5)
        else:
            self.actv(out, in_, AF.Sqrt, bias=eps, scale=scale)
            self.recip(out, out)


def make_consts(S):
    bf = ml_dtypes.bfloat16
    pos = np.arange(S)
    hi = (pos // 64).astype(np.float64)
    lo = (pos % 64).astype(np.float64)
    slopes = [2.0 ** (-8.0 * (h + 1) / 4) for h in range(4)]
    aq = np.zeros((4, 4, S), np.float64)
    ak = np.zeros((4, 4, S), np.float64)
    dmat = np.zeros((4, 128, 128), np.float64)
    dd = np.abs(np.arange(128)[:, None] - np.arange(128)[None, :])
    for h, sl in enumerate(slopes):
        aq[h, 0] = 64 * sl * hi
        aq[h, 1] = sl * lo
        aq[h, 2] = 1
        aq[h, 3] = 1
        ak[h, 0] = -1
        ak[h, 1] = -1
        ak[h, 2] = 64 * sl * hi
        ak[h, 3] = sl * lo
        ii = np.arange(128)[:, None]
        jj = np.arange(128)[None, :]
        dmat[h] = np.where(ii > jj, -2.0 * sl * (ii - jj), 0.0)
    edge = np.ones((2, 128, 16), np.float64)
    for g, w in enumerate((2, 4, 8, 16)):
        tl, pr = g // 2, (g % 2) * 64
        for i in range(8):
            t = i
            cnt = min(t + w // 2, S) - max(t - w // 2, 0)
            edge[tl, pr:pr + 64, i] = w / cnt
            t = S - 8 + i
            cnt = min(t + w // 2, S) - max(t - w // 2, 0)
            edge[tl, pr:pr + 64, 8 + i] = w / cnt
    invw = np.zeros((2, 128, 1), np.float64)
    for g, w in enumerate((2, 4, 8, 16)):
        invw[g // 2, (g % 2) * 64:(g % 2) * 64 + 64, 0] = 1.0 / w
    p = np.arange(128)
    blk32 = (p[:, None] // 32 == p[None, :] // 32).astype(np.float64)
    blk64 = (p[:, None] // 64 == p[None, :] // 64).astype(np.float64)
    return {
        "c_aq": aq.astype(bf), "c_aqn": (-aq).astype(bf), "c_ak": ak.astype(bf),
        "c_dmat": dmat.transpose(1, 0, 2).copy().astype(bf),
        "c_edge": edge.transpose(1, 0, 2).copy().astype(np.float32),
        "c_invw": invw.transpose(1, 0, 2).copy().astype(np.float32),
        "c_blk32": blk32.astype(bf), "c_blk64": blk64.astype(bf),
        "c_ident": np.eye(128).astype(bf), "c_identf": np.eye(128).astype(np.float32),
        "c_iota": np.tile(np.arange(128, dtype=np.float32)[None, :], (128, 1)),
    }


PARAM_SHAPES = {
    "norm_mix": (D,), "w_in": (D, 2560), "pool_w": (4, 64, 64), "pool_scale": (GW,), "conv_dw": (31, GW),
    "conv_ln_g": (GW,), "conv_ln_b": (GW,), "conv_pw": (GW, GW), "diff_qk_norm": (2, 32), "diff_lambda": (4, 32),
    "diff_subln": (64,), "sconv_w": (3, GW), "mem_norm": (D,), "w_mem_kv": (D, 512), "mem_qk_norm": (2, 64),
    "group_norm": (1280,), "w_out": (1280, D), "norm_ffn": (D,), "peer_wq": (D, 2048),
    "peer_subkeys": (8, 2, 128, 128), "peer_u": (NEXP, D), "peer_v": (NEXP, D),
}


def build(S=SEQ, depth=DEPTH, with_peer=True):
    nc = bass.Bass("TRN2", target_bir_lowering=False)
    fw = FW(nc)
    NCH = S // 512
    NKT = S // 128
    cs = make_consts(S)

    def ext(name, shape, dt=F32):
        return T(nc.dram_tensor(name, list(shape), dt, kind="ExternalInput").ap(), name)

    x_in = ext("x", (S, D))
    mem_in = ext("mem", (NMEM, D))
    P = {k: ext(k, (depth,) + shp) for k, shp in PARAM_SHAPES.items()}
    C = {k: ext(k, v.shape, BF16 if v.dtype == ml_dtypes.bfloat16 else F32) for k, v in cs.items()}
    out_d = T(nc.dram_tensor("out", [S, D], F32, kind="ExternalOutput").ap(), "out")

    xbuf = fw.dram("xbuf", (S, D), F32)
    xmid = fw.dram("xmid", (S, D), F32)
    projT = fw.dram("projT", (14, 128, S), F32)
    qTd = fw.dram("qTd", (4, 128, S), BF16)
    qmTd = fw.dram("qmTd", (2, 128, S), BF16)
    uT_d = fw.dram("uT_d", (128, 128, 8, 128), BF16)
    v_d = fw.dram("v_d", (NEXP, D), BF16)

    dve, act, pool, pe = fw.dve, fw.act, fw.pool, fw.pe
    stq = fw.act

    ident = fw.sbuf("ident", [128, 128], BF16)
    identf = fw.sbuf("identf", [128, 128], F32)
    ones = fw.sbuf("ones", [128, 128], BF16)
    blk32 = fw.sbuf("blk32", [128, 128], BF16)
    blk64 = fw.sbuf("blk64", [128, 128], BF16)
    dmat = fw.sbuf("dmat", [128, 4, 128], BF16)
    edge = fw.sbuf("edge", [128, 2, 16], F32)
    invw = fw.sbuf("invw", [128, 2, 1], F32)
    iota = fw.sbuf("iota", [128, 128], F32)
    fw.dma(ident[:], C["c_ident"][:])
    fw.dma(identf[:], C["c_identf"][:])
    fw.dma(blk32[:], C["c_blk32"][:])
    fw.dma(blk64[:], C["c_blk64"][:])
    fw.dma(dmat[:], C["c_dmat"][:])
    fw.dma(edge[:], C["c_edge"][:])
    fw.dma(invw[:], C["c_invw"][:])
    fw.dma(iota[:], C["c_iota"][:])
    fw.memset(dve, ones[:], 1.0)
    onesf = fw.sbuf("onesf", [128, 64], F32)
    fw.memset(dve, onesf[:], 1.0)
    ldstg = fw.sbuf("ldstg", [32, 128], F32)

    def load_T(dst, src, n, ps):
        fw.dma(ldstg[0:n, :], src)
        fw.tr(ps[:, 0:n], ldstg[0:n, :], identf[0:n, 0:n])
        fw.copy(dve, dst, ps[:, 0:n])
    cur_x = x_in

    for l in range(depth):
        last = l == depth - 1
        lam_init = 0.8 - 0.6 * math.exp(-0.3 * l)
        dst_x = out_d if last else xbuf

        fw.push()
        kT = [fw.sbuf(f"kT{i}", [128, S], BF16) for i in range(4)]
        Vaug = fw.sbuf("Vaug", [128, NKT, 4, 66], BF16)
        kmT = [fw.sbuf(f"kmT{i}", [128, NMEM], BF16) for i in range(2)]
        Vm = fw.sbuf("Vm", [128, 2, 4, 66], BF16)
        for i in range(4):
            fw.memset(pool, kT[i][:], 0.0)
        fw.memset(dve, Vaug[:], 1.0)
        fw.memset(dve, Vm[:], 1.0)
        for hp in range(4):
            for side in range(2):
                h = 2 * (hp % 2) + side
                fw.dma(kT[hp][side * 64 + 32:side * 64 + 36, :], C["c_ak"][h])

        fw.push()
        W = fw.sbuf("W", [128, 8, 3072], BF16)
        stg = [fw.sbuf(f"stg{i}", [128, 2560], F32) for i in range(2)]
        gmix = fw.sbuf("gmix", [128, 8], F32)
        gmem = fw.sbuf("gmem", [128, 8], F32)
        gq = fw.sbuf("gq", [128, 1], F32)
        gk = fw.sbuf("gk", [128, 1], F32)
        gmq = fw.sbuf("gmq", [128, 1], F32)
        gmk = fw.sbuf("gmk", [128, 1], F32)
        Wkv = fw.sbuf("Wkv", [128, 8, 512], BF16)
        memnT = fw.sbuf("memnT", [128, 8, NMEM], BF16)
        xt = [fw.sbuf(f"xt{i}", [128, D], F32) for i in range(2)]
        xs = [fw.sbuf(f"xs{i}", [128, D], BF16) for i in range(2)]
        junk = fw.sbuf("junk", [128, D], F32)
        ssv = [fw.sbuf(f"ssv{i}", [128, 1], F32) for i in range(2)]
        xnT = [fw.sbuf(f"xnT{i}", [128, 8, 512], BF16) for i in range(2)]
        ev_f = [fw.sbuf(f"evf{i}", [128, 512], F32) for i in range(3)]
        sqb = [fw.sbuf(f"sqb{i}", [128, 512], BF16) for i in range(2)]
        rst = [fw.sbuf(f"rst{i}", [128, 512], F32) for i in range(2)]
        qn = [fw.sbuf(f"qn{i}", [128, 512], BF16) for i in range(2)]
        tp = [fw.psum(f"tp{i}", [128, 8, 128], BF16) for i in range(2)]
        pj = [fw.psum(f"pj{i}", [128, 512], F32) for i in range(3)]
        pss = [fw.psum(f"pss{i}", [128, 512], F32) for i in range(2)]

        fw.memset(pool, W[:], 0.0)
        fw.memset(dve, gq[:], 0.0)
        fw.memset(dve, gk[:], 0.0)
        load_T(gmix[:], P["norm_mix"][l].rr("(kc kp) -> kc kp", kp=128), 8, pj[0])
        load_T(gmem[:], P["mem_norm"][l].rr("(kc kp) -> kc kp", kp=128), 8, pj[1])
        for side in range(2):
            fw.dma(gq[side * 64:side * 64 + 32, :], P["diff_qk_norm"][l, 0].rr("(d o) -> d o", o=1))
            fw.dma(gk[side * 64:side * 64 + 32, :], P["diff_qk_norm"][l, 1].rr("(d o) -> d o", o=1))
            fw.dma(gmq[side * 64:side * 64 + 64, :], P["mem_qk_norm"][l, 0].rr("(d o) -> d o", o=1))
            fw.dma(gmk[side * 64:side * 64 + 64, :], P["mem_qk_norm"][l, 1].rr("(d o) -> d o", o=1))
        fw.ts(dve, gq[:], gq[:], 32 ** -0.5, None, ALU.mult)
        fw.ts(dve, gmq[:], gmq[:], 64 ** -0.5, None, ALU.mult)

        for kc in range(8):
            sg = stg[kc % 2]
            fw.dma(sg[:], P["w_in"][l, kc * 128:(kc + 1) * 128, :])
            e = pool if kc % 2 == 0 else act
            fw.copy(e, W[:, kc, 0:768], sg[:, 0:768])
            for off_in, off_out in ((768, 768), (1024, 1280)):
                fw.copy(e, W[:, kc, off_out:off_out + 512].rr("p (a b c) -> p a b c", a=4, b=2)[:, :, :, 0:32],
                        sg[:, off_in:off_in + 256].rr("p (a b c) -> p a b c", a=4, b=2))
            fw.copy(e, W[:, kc, 1792:3072], sg[:, 1280:2560])
        for kc in range(8):
            sg = stg[kc % 2]
            fw.dma(sg[:, 0:512], P["w_mem_kv"][l, kc * 128:(kc + 1) * 128, :])
            fw.copy(pool if kc % 2 == 0 else act, Wkv[:, kc, :], sg[:, 0:512])

        def norm_T(src_v, i, gcol, dstT, col0):
            fw.dma(xt[i][:], src_v)
            fw.actv(junk[:], xt[i][:], AF.Square, accum=ssv[i][:])
            fw.rsqrt(ssv[i][:], ssv[i][:], 1.0 / D, EPS)
            fw.actv(xs[i][:], xt[i][:], AF.Copy, scale=ssv[i][:])
            for kc in range(8):
                fw.tr(tp[i][:, kc, :], xs[i][:, kc * 128:(kc + 1) * 128], ident[:], signal=(kc == 7))
            fw.tt(dve, dstT[:, :, col0:col0 + 128], tp[i][:], gcol[:].m(lambda a: a.unsqueeze(2).broadcast_to([128, 8, 128])), ALU.mult)

        def qk_norm(ps, blk, gcol, nh, i):
            fw.actv(sqb[i][:], ps[:], AF.Square)
            fw.mm(pss[i][:], blk[:], sqb[i][:])
            fw.rsqrt(rst[i][:], pss[i][:], 1.0 / nh, EPS)

        for mt in range(2):
            norm_T(mem_in[mt * 128:(mt + 1) * 128, :], mt, gmem, memnT, mt * 128)
        for hp in range(2):
            ps = pj[hp]
            for kc in range(8):
                fw.mm(ps[:, 0:NMEM], Wkv[:, kc, hp * 128:(hp + 1) * 128], memnT[:, kc, :], start=(kc == 0), stop=(kc == 7))
            fw.actv(sqb[hp][:, 0:NMEM], ps[:, 0:NMEM], AF.Square)
            fw.mm(pss[hp][:, 0:NMEM], blk64[:], sqb[hp][:, 0:NMEM])
            fw.rsqrt(rst[hp][:, 0:NMEM], pss[hp][:, 0:NMEM], 1.0 / 64, EPS)
            fw.stt(kmT[hp][:], ps[:, 0:NMEM], gmk[:], rst[hp][:, 0:NMEM], ALU.mult, ALU.mult)
        for mt in range(2):
            ps = pj[2]
            for kc in range(8):
                fw.mm(ps[:, 0:256], memnT[:, kc, mt * 128:(mt + 1) * 128], Wkv[:, kc, 256:512], start=(kc == 0), stop=(kc == 7))
            fw.copy(act, Vm[:, mt, :, 0:64], ps[:, 0:256].rr("p (h e) -> p h e", h=4))

        conv_chunks = [0, 128, 256, 384, 512, 640] + [2048 + 128 * i for i in range(6)]
        for c in range(NCH):
            xT = xnT[c % 2]
            for sub in range(4):
                r0 = c * 512 + sub * 128
                norm_T(cur_x[r0:r0 + 128, :], sub % 2, gmix, xT, sub * 128)
            cnt = 0
            for ci, wo in enumerate(conv_chunks):
                ps = pj[cnt % 3]
                evb = ev_f[cnt % 3]
                cnt += 1
                for kc in range(8):
                    fw.mm(ps[:], W[:, kc, wo:wo + 128], xT[:, kc, :], start=(kc == 0), stop=(kc == 7))
                fw.copy(act, evb[:], ps[:])
                fw.dma(projT.k(("p", ci, c))[ci, :, c * 512:(c + 1) * 512], evb[:], q=stq)
            for hp in range(4):
                ps = pj[cnt % 3]
                cnt += 1
                i = hp % 2
                for kc in range(8):
                    fw.mm(ps[:], W[:, kc, 768 + hp * 128:768 + (hp + 1) * 128], xT[:, kc, :], start=(kc == 0), stop=(kc == 7))
                qk_norm(ps, blk32, gq, 32, i)
                fw.stt(qn[i][:], ps[:], gq[:], rst[i][:], ALU.mult, ALU.mult)
                fw.dma(qTd.k(("q", hp, c))[hp, :, c * 512:(c + 1) * 512], qn[i][:], q=stq)
            for hp in range(4):
                ps = pj[cnt % 3]
                cnt += 1
                i = hp % 2
                for kc in range(8):
                    fw.mm(ps[:], W[:, kc, 1280 + hp * 128:1280 + (hp + 1) * 128], xT[:, kc, :], start=(kc == 0), stop=(kc == 7))
                qk_norm(ps, blk32, gk, 32, i)
                for side in range(2):
                    r = slice(side * 64, side * 64 + 32)
                    fw.stt(kT[hp].k(("k", c))[r, c * 512:(c + 1) * 512], ps[r, :], gk[r, :], rst[i][r, :], ALU.mult, ALU.mult)
            for hp in range(2):
                ps = pj[cnt % 3]
                cnt += 1
                i = hp % 2
                for kc in range(8):
                    fw.mm(ps[:], W[:, kc, 2816 + hp * 128:2816 + (hp + 1) * 128], xT[:, kc, :], start=(kc == 0), stop=(kc == 7))
                qk_norm(ps, blk64, gmq, 64, i)
                fw.stt(qn[i][:], ps[:], gmq[:], rst[i][:], ALU.mult, ALU.mult)
                fw.dma(qmTd.k(("q", hp, c))[hp, :, c * 512:(c + 1) * 512], qn[i][:], q=stq)
            for sub in range(4):
                ps = pj[cnt % 3]
                cnt += 1
                for kc in range(8):
                    fw.mm(ps[:, 0:256], xT[:, kc, sub * 128:(sub + 1) * 128], W[:, kc, 1792:2048], start=(kc == 0), stop=(kc == 7))
                fw.copy(act, Vaug.k(("v", c, sub))[:, c * 4 + sub, :, 0:64], ps[:, 0:256].rr("p (h e) -> p h e", h=4))
        fw.pop()

        fw.push()
        Wo = fw.sbuf("Wo", [128, 14, D], BF16)
        stgo = [fw.sbuf(f"stgo{i}", [128, D], F32) for i in range(2)]
        gn = fw.sbuf("gn", [128, 14], F32)
        BD = fw.sbuf("BD", [128, 2, 128], BF16)
        BDf = fw.sbuf("BDf", [128, 2, 128], F32)
        pscale = fw.sbuf("pscale", [128, 2], F32)
        cdw = fw.sbuf("cdw", [128, 2, 31], F32)
        diag = fw.sbuf("diag", [128, 2, 31, 128], BF16)
        lng = fw.sbuf("lng", [128, 2], F32)
        lnb = fw.sbuf("lnb", [128, 2], F32)
        pw = fw.sbuf("pw", [128, 2, 256], BF16)
        scw = fw.sbuf("scw", [128, 2, 3], F32)
        lamt = fw.sbuf("lamt", [64, 4, 32], F32)
        lamp = fw.sbuf("lamp", [64, 2, 32], F32)
        lams = fw.sbuf("lams", [64, 2], F32)
        neglam = fw.sbuf("neglam", [64, 1], F32)
        gsub = fw.sbuf("gsub", [64, 1], F32)

        qA = [[fw.sbuf(f"qA{i}_{sd}", [128, 512], BF16) for sd in range(2)] for i in range(4)]
        qB = [[fw.sbuf(f"qB{i}_{sd}", [128, 512], BF16) for sd in range(2)] for i in range(4)]
        qm = [[fw.sbuf(f"qm{i}_{sd}", [128, 512], BF16) for sd in range(2)] for i in range(2)]
        for lst in (qA, qB, qm):
            for pair in lst:
                for tl in pair:
                    fw.memset(pool, tl[:], 0.0)
        pT = [fw.sbuf(f"pT{i}", [128, 512], BF16) for i in range(3)]
        rd = fw.sbuf("rd", [128, 512], F32)
        bcs = fw.sbuf("bcs", [64, 512], F32)
        oj = [fw.sbuf(f"oj{i}", [64, 512], F32) for i in range(2)]
        yh = [fw.sbuf(f"yh{i}", [64, 512], F32) for i in range(4)]
        sq = [fw.sbuf(f"sq{i}", [128, 512], BF16) for i in range(4)]
        rs = fw.sbuf("rs", [128, 512], F32)
        catT = fw.sbuf("catT", [128, 14, 512], BF16)
        f1 = [fw.sbuf(f"f1_{i}", [128, 544], F32) for i in range(6)]
        f2 = f1[2:6]
        hb = [fw.sbuf(f"hb{i}", [128, 544], BF16) for i in range(2)]
        yb = [fw.sbuf(f"yb{i}", [128, 512], F32) for i in range(2)]
        pb = [fw.sbuf(f"pb{i}", [128, 512], BF16) for i in range(2)]
        xr = [fw.sbuf(f"xr{i}", [128, D], F32) for i in range(2)]

        sps = [fw.psum(f"sps{i}", [128, 512], F32) for i in range(3)]
        ops = [fw.psum(f"ops{i}", [128, 512], F32) for i in range(2)]
        gps = [fw.psum(f"gps{i}", [128, 512], F32) for i in range(3)]

        wo_rows = [(0, 128), (128, 128), (256, 128), (384, 128), (512, 64), (576, 64), (640, 64), (704, 64),
                   (768, 128), (896, 128), (1024, 64), (1088, 64), (1152, 64), (1216, 64)]
        for ci, (r0, n) in enumerate(wo_rows):
            sg = stgo[ci % 2]
            fw.dma(sg[0:n, :], P["w_out"][l, r0:r0 + n, :])
            fw.copy(pool if ci % 2 == 0 else act, Wo[0:n, ci, :], sg[0:n, :])
            fw.dma(gn[0:n, ci:ci + 1], P["group_norm"][l, r0:r0 + n].rr("(d o) -> d o", o=1))
        fw.memset(dve, BDf[:], 0.0)
        for g in range(4):
            pr = (g % 2) * 64
            fw.dma(BDf[pr:pr + 64, g // 2, pr:pr + 64], P["pool_w"][l, g])
        fw.copy(dve, BD[:], BDf[:])
        load_T(pscale[:], P["pool_scale"][l].rr("(t p) -> t p", p=128), 2, gps[0])
        load_T(lng[:], P["conv_ln_g"][l].rr("(t p) -> t p", p=128), 2, gps[1])
        load_T(lnb[:], P["conv_ln_b"][l].rr("(t p) -> t p", p=128), 2, gps[2])
        for t in range(2):
            load_T(cdw[:, t, :], P["conv_dw"][l, :, t * 128:(t + 1) * 128], 31, gps[t])
            load_T(scw[:, t, :], P["sconv_w"][l, :, t * 128:(t + 1) * 128], 3, gps[2])
        for t in range(2):
            fw.dma(stgo[t][:, 0:256], P["conv_pw"][l, t * 128:(t + 1) * 128, :])
            fw.copy(dve, pw[:, t, :], stgo[t][:, 0:256])
            for k in range(31):
                fw.ts(dve if k % 2 == 0 else pool, diag[:, t, k, :], ident[:], cdw[:, t, k:k + 1], None, ALU.mult)
        fw.dma(lamt[:], P["diff_lambda"][l].m(lambda a: a.unsqueeze(0).broadcast_to([64, 4, 32])))
        fw.tt(dve, lamp[:], lamt[:, 0:4:2, :], lamt[:, 1:4:2, :], ALU.mult)
        fw.reduce(lams[:], lamp[:], ALU.add)
        fw.actv(lams[:], lams[:], AF.Exp)
        fw.tt(dve, neglam[:], lams[:, 1:2], lams[:, 0:1], ALU.subtract)
        fw.ts(dve, neglam[:], neglam[:], -lam_init, None, ALU.add)
        fw.dma(gsub[:], P["diff_subln"][l].rr("(d o) -> d o", o=1))
        fw.ts(dve, gsub[:], gsub[:], 1.0 - lam_init, None, ALU.mult)

        def group_norm(ys, cidx, Pn):
            n = len(ys)
            gp = gps[0]
            for i, y in enumerate(ys):
                fw.actv(sq[i][0:Pn, :], y, AF.Square)
            for i in range(n):
                fw.mm(gp[0:Pn, :], ones[0:Pn, 0:Pn], sq[i][0:Pn, :], start=(i == 0), stop=(i == n - 1))
            fw.rsqrt(rs[0:Pn, :], gp[0:Pn, :], 1.0 / GW, EPS)
            for i, y in enumerate(ys):
                fw.stt(catT[0:Pn, cidx[i], :], y, gn[0:Pn, cidx[i]:cidx[i] + 1], rs[0:Pn, :], ALU.mult, ALU.mult)

        pend = []

        def flush():
            while pend:
                pend.pop(0)()

        def attn_head(qa, qb, kt_tile, r0, nd, nkt, Vt, h, qc, dm, out_sb):
            op_ = ops[attn_head.n % 2]
            attn_head.n += 1
            ra = slice(0, 128)
            def qk(kt):
                sp_ = sps[kt % 3]
                kc_ = slice(kt * 128, (kt + 1) * 128)
                rel = kt - 4 * qc
                if dm is None or rel < 0:
                    fw.mm(sp_[:], kt_tile[ra, kc_], qa[ra, :])
                elif rel > 3:
                    fw.mm(sp_[:], kt_tile[ra, kc_], qb[ra, :])
                else:
                    if rel > 0:
                        fw.mm(sp_[:, 0:rel * 128], kt_tile[ra, kc_], qb[ra, 0:rel * 128], signal=False)
                    dc = slice(rel * 128, (rel + 1) * 128)
                    fw.mm(sp_[:, dc], kt_tile[ra, kc_], qa[ra, dc], start=True, stop=False, signal=False)
                    fw.mm(sp_[:, dc], ident[:], dm, start=False, stop=True, signal=(rel == 3))
                    if rel < 3:
                        fw.mm(sp_[:, (rel + 1) * 128:512], kt_tile[ra, kc_], qa[ra, (rel + 1) * 128:512])
                fw.actv(pT[kt % 3][:], sp_[:], AF.Exp)

            qk(0)
            for kt in range(nkt):
                if kt + 1 < nkt:
                    qk(kt + 1)
                fw.mm(op_[0:65, :], Vt[:, kt, h, 0:65], pT[kt % 3][:], start=(kt == 0), stop=(kt == nkt - 1))
                if kt == 1:
                    flush()

            def epi():
                fw.recip(rd[64:65, :], op_[64:65, :])
                bp = gps[1]
                fw.mm(bp[0:64, :], onesf[64:65, 0:64], rd[64:65, :])
                fw.copy(act, bcs[:], bp[0:64, :])
                fw.tt(dve, out_sb, op_[0:64, :], bcs[:], ALU.mult)

            pend.append(epi)

        attn_head.n = 0

        for c in range(NCH):
            c0 = c * 512
            for hp in range(4):
                for side in range(2):
                    h = 2 * (hp % 2) + side
                    rd_ = slice(side * 64, side * 64 + 32)
                    r = slice(side * 64 + 32, side * 64 + 36)
                    fw.dma(qA[hp][side][rd_, :], qTd[hp, rd_, c0:c0 + 512])
                    fw.dma(qB[hp][side][rd_, :], qTd[hp, rd_, c0:c0 + 512])
                    fw.dma(qA[hp][side][r, :], C["c_aq"][h, :, c0:c0 + 512])
                    fw.dma(qB[hp][side][r, :], C["c_aqn"][h, :, c0:c0 + 512])
            for hp in range(2):
                for side in range(2):
                    rm_ = slice(side * 64, side * 64 + 64)
                    fw.dma(qm[hp][side][rm_, :], qmTd[hp, rm_, c0:c0 + 512])

            for h in range(4):
                for j in range(2):
                    hp = j * 2 + h // 2
                    side = h % 2
                    attn_head(qA[hp][side], qB[hp][side], kT[hp], side * 64, 32, NKT, Vaug, h, c, dmat[:, h, :], oj[j][:])
                    if j == 0:
                        pass

                def comb(h=h):
                    fw.stt(yh[h][:], oj[1][:], neglam[:], oj[0][:], ALU.mult, ALU.add)
                    fw.actv(sq[0][0:64, :], yh[h][:], AF.Square)
                    fw.mm(gps[2][0:64, :], ones[0:64, 0:64], sq[0][0:64, :])
                    fw.rsqrt(rs[0:64, :], gps[2][0:64, :], 1.0 / 64, EPS)
                    fw.stt(yh[h][:], yh[h][:], gsub[:], rs[0:64, :], ALU.mult, ALU.mult)

                pend.append(comb)
            flush()
            group_norm([yh[h][:] for h in range(4)], [4, 5, 6, 7], 64)

            for h in range(4):
                hp, side = h // 2, h % 2
                attn_head(qm[hp][side], qm[hp][side], kmT[hp], side * 64, 64, 2, Vm, h, c, None, yh[h][:])
            flush()
            group_norm([yh[h][:] for h in range(4)], [10, 11, 12, 13], 64)

            lo = max(c0 - 8, 0)
            hi_ = min(c0 + 520, S)
            ys = []
            for t in range(2):
                u = f1[t]
                if c == 0 or c == NCH - 1:
                    fw.memset(dve, u[:, 0:528], 0.0)
                fw.dma(u[:, lo - (c0 - 8):hi_ - (c0 - 8)], projT[t, :, lo:hi_])
                s2, s4, s8, s16 = f1[2], f1[3], f1[4], f1[5]
                fw.tt(dve, s2[:, 1:528], u[:, 0:527], u[:, 1:528], ALU.add)
                fw.tt(dve, s4[:, 2:527], s2[:, 1:526], s2[:, 3:528], ALU.add)
                fw.tt(dve, s8[:, 4:525], s4[:, 2:523], s4[:, 6:527], ALU.add)
                fw.tt(dve, s16[:, 8:520], s8[:, 4:516], s8[:, 12:524], ALU.add)
                srcs = (s2, s4) if t == 0 else (s8, s16)
                for g2 in range(2):
                    pr = slice(g2 * 64, g2 * 64 + 64)
                    sw = srcs[g2]
                    if c == 0:
                        fw.tt(dve, sw[pr, 8:16], sw[pr, 8:16], edge[pr, t, 0:8], ALU.mult)
                    if c == NCH - 1:
                        fw.tt(dve, sw[pr, 512:520], sw[pr, 512:520], edge[pr, t, 8:16], ALU.mult)
                    fw.stt(pb[t][pr, :], sw[pr, 8:520], invw[pr, t, :], u[pr, 8:520], ALU.mult, ALU.subtract)
                fw.mm(gps[2][:], BD[:, t, :], pb[t][:])
                fw.actv(yb[t][:], gps[2][:], AF.Copy, scale=pscale[:, t:t + 1])
                ys.append(yb[t][:])
            group_norm(ys, [0, 1], 128)

            lo = max(c0 - 1, 0)
            hi_ = min(c0 + 513, S)
            ys = []
            for t in range(2):
                hh, bb, cc, z, cv = f1[0], f1[1], f1[2], f1[3], f1[4]
                if c == 0 or c == NCH - 1:
                    fw.memset(dve, hh[:, 0:514], 0.0)
                    fw.memset(dve, cc[:, 0:514], 0.0)
                fw.dma(hh[:, lo - (c0 - 1):hi_ - (c0 - 1)], projT[6 + t, :, lo:hi_])
                fw.dma(bb[:, 0:512], projT[8 + t, :, c0:c0 + 512])
                fw.dma(cc[:, lo - (c0 - 1):hi_ - (c0 - 1)], projT[10 + t, :, lo:hi_])
                fw.tt(dve, z[:, 0:514], hh[:, 0:514], cc[:, 0:514], ALU.mult)
                fw.ts(dve, cv[:, 0:512], z[:, 0:512], scw[:, t, 0:1], None, ALU.mult)
                fw.stt(cv[:, 0:512], z[:, 1:513], scw[:, t, 1:2], cv[:, 0:512], ALU.mult, ALU.add)
                fw.stt(cv[:, 0:512], z[:, 2:514], scw[:, t, 2:3], cv[:, 0:512], ALU.mult, ALU.add)
                fw.tt(dve, yb[t][:], bb[:, 0:512], cv[:, 0:512], ALU.mult)
                ys.append(yb[t][:])
            group_norm(ys, [8, 9], 128)

            lo = max(c0 - 15, 0)
            hi_ = min(c0 + 527, S)
            cvs = []
            for t in range(2):
                a_, g_ = f2[t * 2], f2[t * 2 + 1]
                if c == 0 or c == NCH - 1:
                    fw.memset(dve, a_[:, 0:542], 0.0)
                    fw.memset(dve, g_[:, 0:542], 0.0)
                fw.dma(a_[:, lo - (c0 - 15):hi_ - (c0 - 15)], projT[2 + t, :, lo:hi_])
                fw.dma(g_[:, lo - (c0 - 15):hi_ - (c0 - 15)], projT[4 + t, :, lo:hi_])
                fw.actv(g_[:, 0:542], g_[:, 0:542], AF.Sigmoid)
                fw.tt(dve, hb[t][:, 0:542], a_[:, 0:542], g_[:, 0:542], ALU.mult)
                cp = sps[t]
                for k in range(31):
                    fw.mm(cp[:], diag[:, t, k, :], hb[t][:, k:k + 512], start=(k == 0), stop=(k == 30))
                cvs.append(cp)
            for t in range(2):
                fw.copy(act, pb[t][:], cvs[t][:])
                fw.actv(sq[t][:], cvs[t][:], AF.Square)
            for t in range(2):
                fw.mm(gps[0][:], ones[:], pb[t][:], start=(t == 0), stop=(t == 1))
            for t in range(2):
                fw.mm(gps[1][:], ones[:], sq[t][:], start=(t == 0), stop=(t == 1))
            mu, var = f1[0], f1[1]
            fw.ts(dve, mu[:, 0:512], gps[0][:], 1.0 / GW, None, ALU.mult)
            fw.tt(dve, var[:, 0:512], mu[:, 0:512], mu[:, 0:512], ALU.mult)
            fw.stt(var[:, 0:512], gps[1][:], 1.0 / GW, var[:, 0:512], ALU.mult, ALU.subtract)
            fw.ts(dve, var[:, 0:512], var[:, 0:512], 0.0, None, ALU.max)
            fw.rsqrt(var[:, 0:512], var[:, 0:512], 1.0, EPS)
            for t in range(2):
                xc = f1[2 + t]
                fw.tt(dve, xc[:, 0:512], cvs[t][:], mu[:, 0:512], ALU.subtract)
                fw.tt(dve, xc[:, 0:512], xc[:, 0:512], var[:, 0:512], ALU.mult)
                fw.actv(hb[t][:, 0:512], xc[:, 0:512], AF.Silu, bias=lnb[:, t:t + 1], scale=lng[:, t:t + 1])
            ys = []
            for dt_ in range(2):
                yp = gps[2]
                for t in range(2):
                    fw.mm(yp[:], pw[:, t, dt_ * 128:(dt_ + 1) * 128], hb[t][:, 0:512], start=(t == 0), stop=(t == 1))
                fw.copy(act, yb[dt_][:], yp[:])
                ys.append(yb[dt_][:])
            group_norm(ys, [2, 3], 128)

            for sub in range(4):
                xx = xr[sub % 2]
                r0 = c0 + sub * 128
                fw.dma(xx[:], cur_x[r0:r0 + 128, :])
                for dh in range(2):
                    op_ = ops[dh]
                    for ci, (_, n) in enumerate(wo_rows):
                        fw.mm(op_[:], catT[0:n, ci, sub * 128:(sub + 1) * 128], Wo[0:n, ci, dh * 512:(dh + 1) * 512],
                              start=(ci == 0), stop=(ci == 13))
                    fw.tt(dve, xx[:, dh * 512:(dh + 1) * 512], xx[:, dh * 512:(dh + 1) * 512], op_[:], ALU.add)
                dstm = xmid if with_peer else dst_x
                fw.dma(dstm.k(("x", c, sub))[r0:r0 + 128, :], xx[:], q=stq)
        fw.pop()
        fw.pop()

        if with_peer:
            peer_phase(fw, nc, S, l, P, xmid, dst_x, ident, identf, iota, uT_d, v_d)
        cur_x = dst_x

    fw.barrier()
    fw.close()
    return nc, cs


def peer_phase(fw, nc, S, l, P, xmid, dst_x, ident, identf, iota, uT_d, v_d):
    import os
    STAGE = int(os.environ.get("PEER_STAGE", "9"))
    dve, act, pool, pe = fw.dve, fw.act, fw.pool, fw.pe
    stq = fw.act
    TCH = 256
    NTC = S // TCH
    NEG = -1e30

    fw.push()
    uf = [fw.sbuf(f"uf{i}", [128, D], F32) for i in range(2)]
    ub = [fw.sbuf(f"ub{i}", [128, 8, 128], BF16) for i in range(2)]
    vf = [fw.sbuf(f"vf{i}", [128, D], F32) for i in range(2)]
    vb = [fw.sbuf(f"vb{i}", [128, D], BF16) for i in range(2)]
    pt = [fw.psum(f"pt{i}", [128, 8, 128], F32) for i in range(2)]
    for c in range(128):
        i = c % 2
        fw.dma(uf[i][:], P["peer_u"][l, c * 128:(c + 1) * 128, :])
        for kc in range(8):
            fw.tr(pt[i][:, kc, :], uf[i][:, kc * 128:(kc + 1) * 128], identf[:], signal=(kc == 7))
        fw.copy(act, ub[i][:], pt[i][:])
        fw.dma(uT_d.k(("u", c))[c], ub[i][:], q=stq)
        fw.dma(vf[i][:], P["peer_v"][l, c * 128:(c + 1) * 128, :])
        fw.copy(pool, vb[i][:], vf[i][:])
        fw.dma(v_d.k(("v", c))[c * 128:(c + 1) * 128, :], vb[i][:], q=stq)
    fw.pop()
    if STAGE <= 1:
        fw.dma(dst_x[:], xmid[:])
        return

    fw.push()
    Wq = fw.sbuf("Wq", [128, 8, 2048], BF16)
    skT = fw.sbuf("skT", [128, 16, 128], BF16)
    gffn = fw.sbuf("gffn", [128, 8], F32)
    ldst = fw.sbuf("ldst", [8, 128], F32)
    bufA = fw.sbuf("bufA", [128, 2048], F32)
    bufB = fw.sbuf("bufB", [128, 2048], F32)
    bufC = fw.sbuf("bufC", [128, 2048], F32)
    xt2 = [fw.sbuf(f"xt2_{i}", [128, D], F32) for i in range(2)]
    xs2 = fw.sbuf("xs2", [128, D], BF16)
    ssv = fw.sbuf("ssv2", [128, 1], F32)
    xn2T = fw.sbuf("xn2T", [128, 8, TCH], BF16)
    qT = fw.sbuf("qT", [128, 16, TCH], BF16)
    a16 = fw.sbuf("a16", [128, 256], F32)
    ia = fw.sbuf("ia", [128, 256], U32)
    iaf = fw.sbuf("iaf", [128, 256], F32)
    best = fw.sbuf("best", [128, 128], F32)
    ci = fw.sbuf("ci", [128, 128], U32)
    eb = fw.sbuf("eb", [128, 128], F32)
    zs = fw.sbuf("zs", [128, 8], F32)
    gg = fw.sbuf("gg", [128, 128], F32)
    r1i = fw.sbuf("r1i", [128, 128], U32)
    r2i = fw.sbuf("r2i", [128, 128], U32)
    r1f = fw.sbuf("r1f", [128, 128], F32)
    r2f = fw.sbuf("r2f", [128, 128], F32)
    i1s = fw.sbuf("i1s", [128, 128], F32)
    i2s = fw.sbuf("i2s", [128, 128], F32)
    tT = fw.sbuf("tT", [128, 3, 128], F32)
    OHA = fw.sbuf("OHA", [128, 32, 128], BF16)
    OHB = fw.sbuf("OHB", [128, 32, 128], BF16)
    GT = fw.sbuf("GT", [128, TCH, 128], BF16)
    iota_bf = fw.sbuf("iota_bf", [128, 128], BF16)
    fw.copy(dve, iota_bf[:], iota[:])
    utb = [fw.sbuf(f"utb{i}", [128, 2, 8, 128], BF16) for i in range(2)]
    vtb = [fw.sbuf(f"vtb{i}", [128, 2, D], BF16) for i in range(2)]
    geb = [fw.sbuf(f"geb{i}", [128, TCH], BF16) for i in range(2)]
    wt = [fw.sbuf(f"wt{i}", [128, TCH], BF16) for i in range(2)]
    oacc = [fw.psum(f"oacc{i}", [128, D], F32) for i in range(2)]
    pbig = fw.psum("pbig", [128, 2048], F32)

    def bank(i):
        return pbig.k(("b", i))[:, i * 512:(i + 1) * 512]

    fw.dma(ldst[:], P["norm_ffn"][l].rr("(kc kp) -> kc kp", kp=128))
    fw.tr(bank(0)[:, 0:8], ldst[:], identf[0:8, 0:8])
    fw.copy(dve, gffn[:], bank(0)[:, 0:8])
    for kc in range(8):
        sg = bufA if kc % 2 == 0 else bufB
        fw.dma(sg[:], P["peer_wq"][l, kc * 128:(kc + 1) * 128, :])
        fw.copy(pool if kc % 2 == 0 else act, Wq[:, kc, :], sg[:])
    for hj in range(16):
        sg = bufC
        fw.dma(sg.k(hj % 4)[:, (hj % 4) * 128:(hj % 4 + 1) * 128], P["peer_subkeys"][l, hj // 2, hj % 2])
        b = bank(hj % 4)
        fw.tr(b[:, 0:128], sg.k(hj % 4)[:, (hj % 4) * 128:(hj % 4 + 1) * 128], identf[:])
        fw.copy(act if hj % 2 == 0 else dve, skT[:, hj, :], b[:, 0:128])

    if STAGE <= 2:
        fw.dma(dst_x[:], xmid[:])
        fw.pop()
        return
    a16v = a16[:].rr("p (h j r) -> p h j r", h=8, j=2)
    iafv = iaf[:].rr("p (h j r) -> p h j r", h=8, j=2)

    for tc in range(NTC):
        t0 = tc * TCH
        for sub in range(2):
            r0 = t0 + sub * 128
            fw.dma(xt2[sub][:], xmid[r0:r0 + 128, :])
            fw.actv(bufA[:, 0:D], xt2[sub][:], AF.Square, accum=ssv[:])
            fw.rsqrt(ssv[:], ssv[:], 1.0 / D, EPS)
            fw.actv(xs2[:], xt2[sub][:], AF.Copy, scale=ssv[:])
            tpb = bank(sub).m(lambda a: a.bitcast(BF16).rearrange("p (k t) -> p k t", k=8))
            for kc in range(8):
                fw.tr(tpb[:, kc, :], xs2[:, kc * 128:(kc + 1) * 128], ident[:], signal=(kc == 7))
            fw.tt(dve, xn2T[:, :, sub * 128:(sub + 1) * 128], tpb,
                  gffn[:].m(lambda a: a.unsqueeze(2).broadcast_to([128, 8, 128])), ALU.mult)
        for hj in range(16):
            b = bank(hj % 4)
            for kc in range(8):
                fw.mm(b[:, 0:TCH], Wq[:, kc, hj * 128:(hj + 1) * 128], xn2T[:, kc, :], start=(kc == 0), stop=(kc == 7))
            fw.copy(act if hj % 2 == 0 else dve, qT[:, hj, :], b[:, 0:TCH])
        for sub in range(2 if STAGE >= 4 else 0):
            tsl = slice(sub * 128, (sub + 1) * 128)
            for hj in range(16):
                fw.mm(pbig.k(("b", hj // 4))[:, hj * 128:(hj + 1) * 128], qT[:, hj, tsl], skT[:, hj, :])
            for i in range(4):
                fw.copy(act if i % 2 == 0 else dve, bufA.k(i)[:, i * 512:(i + 1) * 512], bank(i))
            def _g1(hj):
                sl = slice(hj * 128, (hj + 1) * 128)
                return (bufA.k(hj // 4)[:, sl], bufB.k(hj)[:, sl], a16.k(hj)[:, hj * 16:hj * 16 + 8], a16.k(hj)[:, hj * 16 + 8:hj * 16 + 16],
                        ia.k(hj)[:, hj * 16:hj * 16 + 8], ia.k(hj)[:, hj * 16 + 8:hj * 16 + 16])
            G1 = [_g1(hj) for hj in range(16)]
            for (sA, sB, a0, a1, i0, i1_) in G1:
                fw.op(dve, lambda: nc.vector.max(a0.ap, sA.ap), [sA], [a0])
            for (sA, sB, a0, a1, i0, i1_) in G1:
                fw.op(dve, lambda: nc.vector.match_replace(sB.ap, a0.ap, sA.ap, NEG), [sA, a0], [sB])
            for (sA, sB, a0, a1, i0, i1_) in G1:
                fw.op(dve, lambda: nc.vector.max_index(i0.ap, a0.ap, sA.ap), [sA, a0], [i0])
            for (sA, sB, a0, a1, i0, i1_) in G1:
                fw.op(dve, lambda: nc.vector.max(a1.ap, sB.ap), [sB], [a1])
            for (sA, sB, a0, a1, i0, i1_) in G1:
                fw.op(dve, lambda: nc.vector.max_index(i1_.ap, a1.ap, sB.ap), [sB, a1], [i1_])
            cand = bufB[:].rr("p (h a b) -> p h a b", h=8, a=16)
            fw.tt(dve, cand, a16v[:, :, 0, :].m(lambda a: a.unsqueeze(3).broadcast_to([128, 8, 16, 16])),
                  a16v[:, :, 1, :].m(lambda a: a.unsqueeze(2).broadcast_to([128, 8, 16, 16])), ALU.add)
            def _g2(h):
                sl = slice(h * 256, (h + 1) * 256)
                return (bufB.k(("c", h))[:, sl], bufC.k(("c", h))[:, sl], best.k(h)[:, h * 16:h * 16 + 8], best.k(h)[:, h * 16 + 8:h * 16 + 16],
                        ci.k(h)[:, h * 16:h * 16 + 8], ci.k(h)[:, h * 16 + 8:h * 16 + 16])
            G2 = [_g2(h) for h in range(8)]
            for (sA, sB, a0, a1, i0, i1_) in G2:
                fw.op(dve, lambda: nc.vector.max(a0.ap, sA.ap), [sA], [a0])
            for (sA, sB, a0, a1, i0, i1_) in G2:
                fw.op(dve, lambda: nc.vector.match_replace(sB.ap, a0.ap, sA.ap, NEG), [sA, a0], [sB])
            for (sA, sB, a0, a1, i0, i1_) in G2:
                fw.op(dve, lambda: nc.vector.max_index(i0.ap, a0.ap, sA.ap), [sA, a0], [i0])
            for (sA, sB, a0, a1, i0, i1_) in G2:
                fw.op(dve, lambda: nc.vector.max(a1.ap, sB.ap), [sB], [a1])
            for (sA, sB, a0, a1, i0, i1_) in G2:
                fw.op(dve, lambda: nc.vector.max_index(i1_.ap, a1.ap, sB.ap), [sB, a1], [i1_])
            bv = best[:].rr("p (h s) -> p h s", h=8)
            ev_ = eb[:].rr("p (h s) -> p h s", h=8)
            fw.tt(dve, ev_, bv, bv[:, :, 0:1].m(lambda a: a.broadcast_to([128, 8, 16])), ALU.subtract)
            fw.actv(eb[:], eb[:], AF.Exp)
            fw.reduce(zs[:], ev_, ALU.add)
            fw.recip(zs[:], zs[:])
            fw.tt(dve, gg[:].rr("p (h s) -> p h s", h=8), ev_, zs[:].m(lambda a: a.unsqueeze(2).broadcast_to([128, 8, 16])), ALU.mult)
            fw.copy(dve, iaf[:], ia[:])
            fw.ts(dve, r1i[:], ci[:], 4, None, ALU.logical_shift_right)
            fw.ts(dve, r2i[:], ci[:], 15, None, ALU.bitwise_and)
            fw.copy(dve, r1f[:], r1i[:])
            fw.copy(dve, r2f[:], r2i[:])
            for (rf, jj, dst) in ((r1f, 0, i1s), (r2f, 1, i2s)):
                oh = bufA[:].rr("p (a r) -> p a r", r=16)
                fw.tt(dve, oh, iota[:, 0:16].m(lambda a: a.unsqueeze(1).broadcast_to([128, 128, 16])),
                      rf[:].m(lambda a: a.unsqueeze(2).broadcast_to([128, 128, 16])), ALU.is_equal)
                oh4 = bufA[:].rr("p (h s r) -> p h s r", h=8, s=16)
                fw.tt(dve, oh4, oh4, iafv[:, :, jj, :].m(lambda a: a.unsqueeze(2).broadcast_to([128, 8, 16, 16])), ALU.mult)
                fw.reduce(dst[:], oh, ALU.add)
            b = bank(0)
            for i, src in enumerate((i1s, i2s, gg)):
                fw.tr(b[:, i * 128:(i + 1) * 128], src[:], identf[:], signal=(i == 2))
            fw.copy(act, tT[:], b[:, 0:384].rr("p (a t) -> p a t", a=3))
            for g4 in range(4):
                ts_ = slice(g4 * 32, (g4 + 1) * 32)
                for tk in range(32):
                    tcol = g4 * 32 + tk
                    fw.ts(dve, OHA.k(tk)[:, tk, :], iota_bf[:], tT[:, 0, tcol:tcol + 1], tT[:, 2, tcol:tcol + 1], ALU.is_equal, ALU.mult)
                    fw.ts(dve, OHB.k(tk)[:, tk, :], iota_bf[:], tT[:, 1, tcol:tcol + 1], None, ALU.is_equal)
                for q4 in range(8):
                    b = bank(1 + q4 % 2)
                    for k in range(4):
                        tk = q4 * 4 + k
                        fw.mm(b[:, k * 128:(k + 1) * 128], OHB.k(tk)[:, tk, :], OHA.k(tk)[:, tk, :], signal=(k == 3))
                    tg = sub * 128 + g4 * 32 + q4 * 4
                    fw.copy(act, GT[:, tg:tg + 4, :], b[:].rr("p (t c) -> p t c", t=4))
        def dense_U(c):
            cg, cc = c // 2, c % 2
            i = cg % 2
            if cc == 0:
                fw.dma(utb[i][:], uT_d[cg * 2:(cg + 1) * 2].rr("c p k e -> p c k e"))
                fw.dma(vtb[i][:], v_d[cg * 256:(cg + 1) * 256, :].rr("(c p) d -> p c d", p=128))
            b = bank(c % 2 + 2)
            for kc in range(8):
                fw.mm(b[:, 0:TCH], utb[i][:, cc, kc, :], xn2T[:, kc, :], start=(kc == 0), stop=(kc == 7))
            fw.actv(geb[c % 2][:], b[:, 0:TCH], AF.Gelu)
            fw.tt(dve, wt[c % 2][:], geb[c % 2][:], GT[:, :, c], ALU.mult)

        def dense_V(c):
            cg, cc = c // 2, c % 2
            i = cg % 2
            for sub in range(2):
                for dh in range(2):
                    fw.mm(oacc[sub].k(dh)[:, dh * 512:(dh + 1) * 512], wt[c % 2][:, sub * 128:(sub + 1) * 128],
                          vtb[i][:, cc, dh * 512:(dh + 1) * 512], start=(c == 0), stop=(c == 127),
                          signal=(sub == 1 and dh == 1))

        if STAGE >= 5:
            dense_U(0)
            for c in range(128):
                if c + 1 < 128:
                    dense_U(c + 1)
                dense_V(c)
        for sub in range(2):
            r0 = t0 + sub * 128
            for dh in range(2 if STAGE >= 5 else 0):
                fw.tt(dve, xt2[sub][:, dh * 512:(dh + 1) * 512], xt2[sub][:, dh * 512:(dh + 1) * 512],
                      oacc[sub].k(dh)[:, dh * 512:(dh + 1) * 512], ALU.add)
            fw.dma(dst_x.k(("o", tc, sub))[r0:r0 + 128, :], xt2[sub][:], q=stq)
    fw.pop()


_CACHE = {}


def kernel(**inputs):
    n = 8
    if "nc" not in _CACHE:
        _CACHE["nc"] = build()
    nc, cs = _CACHE["nc"]
    in_maps = []
    for b in range(n):
        m = {"x": np.ascontiguousarray(inputs["x"][b]), "mem": np.ascontiguousarray(inputs["mem"][b])}
        for k in PARAM_SHAPES:
            m[k] = np.ascontiguousarray(inputs[k])
        m.update(cs)
        in_maps.append(m)
    res = run_bass_kernel_spmd(nc, in_maps, core_ids=list(range(n)))
    return np.stack([r["out"] for r in res.results], axis=0).astype(np.float32)
```
